# Optimizing a Trainium2 kernel written in Bass

```python
import math
import jax, jax.numpy as jnp
from jax import lax
import numpy as np

D_MODEL = 1024
BATCH = 8
SEQ = 2048
DEPTH = 2
DEC_BATCH = 128
DEC_SEQ = 8
PAST_LEN = 2048
PAGE_SIZE = 128

N_META = 16
RW_WIDTH = D_MODEL // 2
RW_HEAD = 64
RW_HEADS = RW_WIDTH // RW_HEAD
DECAY_LORA = 64
ICLR_LORA = 64
GN_EPS = 64e-5
DA_WIDTH = D_MODEL // 2
DA_QK = 64
DA_HEADS = DA_WIDTH // (2 * DA_QK)
DA_V = 2 * DA_QK
N_BUCKETS = 32
MAX_DISTANCE = 128
Q_BLOCK = 128
NORM_EPS = 1e-6
NEG_INF = -1e30

RW_R0 = 0
RW_K0 = RW_R0 + RW_WIDTH
RW_V0 = RW_K0 + RW_WIDTH
RW_WD0 = RW_V0 + RW_WIDTH
RW_AD0 = RW_WD0 + DECAY_LORA
RW_Z0 = RW_AD0 + ICLR_LORA
RW_COLS = RW_Z0 + RW_WIDTH
DA_Q0 = RW_COLS
DA_K0 = DA_Q0 + DA_WIDTH
DA_V0 = DA_K0 + DA_WIDTH
DA_Z0 = DA_V0 + DA_HEADS * DA_V
GA0 = DA_Z0 + DA_WIDTH
GB0 = GA0 + D_MODEL
N_COLS = GB0 + D_MODEL

kernel_name = "rwkv7_diffattn_gated_hybrid_step"


def rms_norm(x, g, eps=NORM_EPS):
    xf = x.astype(jnp.float32)
    y = xf * lax.rsqrt(jnp.mean(xf * xf, axis=-1, keepdims=True) + eps)
    return (y * g.astype(jnp.float32)).astype(x.dtype)


def lambda_init(layer):
    return 0.8 - 0.6 * math.exp(-0.3 * layer)


def t5_bucket(q_pos, k_pos):
    n = jnp.maximum(q_pos[:, None] - k_pos[None, :], 0)
    max_exact = N_BUCKETS // 2
    nf = jnp.maximum(n, 1).astype(jnp.float32)
    large = max_exact + (jnp.log(nf / max_exact) / math.log(MAX_DISTANCE / max_exact)
                         * (N_BUCKETS - max_exact)).astype(jnp.int32)
    large = jnp.minimum(large, N_BUCKETS - 1)
    return jnp.where(n < max_exact, n, large)


def diff_attend(q, k, v, q_pos, k_pos, rel_bias, lam):
    s = jnp.einsum('bqhmd,bkhmd->bhmqk', q, k).astype(jnp.float32) * (DA_QK ** -0.5)
    bias = rel_bias[t5_bucket(q_pos, k_pos)].astype(jnp.float32)
    s = s + jnp.transpose(bias, (2, 0, 1))[None, :, None]
    mask = k_pos[None, :] <= q_pos[:, None]
    s = jnp.where(mask, s, NEG_INF)
    p = jax.nn.softmax(s, axis=-1)
    attn = p[:, :, 0] - lam * p[:, :, 1]
    return jnp.einsum('bhqk,bkhe->bqhe', attn.astype(v.dtype), v)


def diff_attend_blocked(q, k, v, q_pos, k_pos, rel_bias, lam):
    B, Tq = q.shape[0], q.shape[1]
    if Tq <= Q_BLOCK:
        return diff_attend(q, k, v, q_pos, k_pos, rel_bias, lam)
    nb = -(-Tq // Q_BLOCK)
    pad = nb * Q_BLOCK - Tq
    qp = jnp.pad(q, ((0, 0), (0, pad), (0, 0), (0, 0), (0, 0)))
    pp = jnp.pad(q_pos, (0, pad), mode='edge')
    qb = jnp.moveaxis(qp.reshape(B, nb, Q_BLOCK, DA_HEADS, 2, DA_QK), 1, 0)
    pb = pp.reshape(nb, Q_BLOCK)
    out = lax.map(lambda a: diff_attend(a[0], k, v, a[1], k_pos, rel_bias, lam), (qb, pb))
    out = jnp.moveaxis(out, 0, 1).reshape(B, nb * Q_BLOCK, DA_HEADS, DA_V)
    return out[:, :Tq]


def rwkv7_branch(h, s0, w0, w2, a0, a2, k_k, k_a, r_k, lnx_g, lnx_b):
    f32 = jnp.float32
    B, T, _ = h.shape
    r = h[..., RW_R0:RW_K0].astype(f32)
    k = h[..., RW_K0:RW_V0].astype(f32)
    v = h[..., RW_V0:RW_WD0].astype(f32)
    wd = h[..., RW_WD0:RW_AD0]
    ad = h[..., RW_AD0:RW_Z0]
    z = h[..., RW_Z0:RW_COLS].astype(f32)
    w_log = -jax.nn.softplus(-(w0 + jnp.tanh(wd) @ w2).astype(f32)) - 0.5
    decay = jnp.exp(-jnp.exp(w_log))
    a = jax.nn.sigmoid((a0 + ad @ a2).astype(f32))
    hd = lambda t: t.reshape(B, T, RW_HEADS, RW_HEAD)
    kk = hd(k * k_k.astype(f32))
    kk = kk / jnp.maximum(jnp.sqrt(jnp.sum(kk * kk, axis=-1, keepdims=True)), 1e-12)
    k = k * (1.0 + (a - 1.0) * k_a.astype(f32))
    r_, w_, k_, v_, a_ = hd(r), hd(decay), hd(k), hd(v), hd(a)
    rem = -kk
    rep = kk * a_

    def step(S, inp):
        rt, wt, kt, vt, at, bt = inp
        sa = jnp.einsum('bhij,bhj->bhi', S, at)
        S = S * wt[:, :, None, :] + sa[..., None] * bt[:, :, None, :] + vt[..., None] * kt[:, :, None, :]
        return S, jnp.einsum('bhij,bhj->bhi', S, rt)

    tm = lambda t: jnp.swapaxes(t, 0, 1)
    S, y = lax.scan(step, s0.astype(f32), (tm(r_), tm(w_), tm(k_), tm(v_), tm(rem), tm(rep)))
    y = tm(y)
    mean = jnp.mean(y, axis=-1, keepdims=True)
    var = jnp.mean(jnp.square(y - mean), axis=-1, keepdims=True)
    y = ((y - mean) * lax.rsqrt(var + GN_EPS)).reshape(B, T, RW_WIDTH)
    y = y * lnx_g.astype(f32) + lnx_b.astype(f32)
    bonus = jnp.sum(r_ * k_ * r_k.astype(f32), axis=-1, keepdims=True) * v_
    y = (y + bonus.reshape(B, T, RW_WIDTH)) * jax.nn.silu(z)
    return y.astype(h.dtype), S


def trunk(x, pos, shift0, state0, cache_k, cache_v, page_table, rel_bias, norm_g, w_in, mu_shift,
          w0, w2, a0, a2, k_k, k_a, r_k, lnx_g, lnx_b, q_norm_g, k_norm_g,
          lam_q1, lam_k1, lam_q2, lam_k2, subln_g, w_a_out, w_b_out, w_o):
    B, T, _ = x.shape
    new_k, new_v, new_s, new_shift = [], [], [], []
    for l in range(DEPTH):
        xn = rms_norm(x, norm_g[l])
        proj = xn @ w_in[l]
        p_rw = proj[..., :RW_COLS]
        if shift0 is None:
            p_prev = jnp.zeros((B, RW_COLS), p_rw.dtype)
            s_init = jnp.zeros((B, RW_HEADS, RW_HEAD, RW_HEAD), jnp.float32)
        else:
            p_prev = shift0[l].astype(xn.dtype) @ w_in[l][:, :RW_COLS]
            s_init = state0[l]
        shifted = jnp.concatenate([p_prev[:, None], p_rw[:, :-1]], axis=1)
        h = p_rw + (shifted - p_rw) * mu_shift[l]
        y_a, S = rwkv7_branch(h, s_init, w0[l], w2[l], a0[l], a2[l], k_k[l], k_a[l], r_k[l],
                              lnx_g[l], lnx_b[l])
        q = rms_norm(proj[..., DA_Q0:DA_K0].reshape(B, T, DA_HEADS, 2, DA_QK), q_norm_g[l])
        k = rms_norm(proj[..., DA_K0:DA_V0].reshape(B, T, DA_HEADS, 2, DA_QK), k_norm_g[l])
        v = proj[..., DA_V0:DA_Z0].reshape(B, T, DA_HEADS, DA_V)
        if cache_k is None:
            kf, vf, k_pos = k, v, pos
        else:
            past = page_table.shape[1] * PAGE_SIZE
            kp = cache_k[l, page_table].reshape(B, past, DA_HEADS, 2, DA_QK).astype(k.dtype)
            vp = cache_v[l, page_table].reshape(B, past, DA_HEADS, DA_V).astype(v.dtype)
            kf = jnp.concatenate([kp, k], axis=1)
            vf = jnp.concatenate([vp, v], axis=1)
            k_pos = jnp.arange(past + T)
        f32 = jnp.float32
        li = lambda_init(l)
        lam = (jnp.exp(jnp.sum(lam_q1[l].astype(f32) * lam_k1[l].astype(f32)))
               - jnp.exp(jnp.sum(lam_q2[l].astype(f32) * lam_k2[l].astype(f32))) + li)
        o = diff_attend_blocked(q, kf, vf, pos, k_pos, rel_bias, lam)
        o = rms_norm(o, subln_g[l]) * (1.0 - li)
        y_b = o.reshape(B, T, DA_HEADS * DA_V) * jax.nn.silu(proj[..., DA_Z0:GA0])
        m = (jax.nn.sigmoid(proj[..., GA0:GB0]) * (y_a @ w_a_out[l])
             + jax.nn.sigmoid(proj[..., GB0:N_COLS]) * (y_b @ w_b_out[l]))
        x = x + m @ w_o[l]
        new_k.append(k.reshape(B, T, DA_HEADS, 2 * DA_QK))
        new_v.append(v)
        new_s.append(S.astype(x.dtype))
        new_shift.append(xn[:, -1])
    return x, jnp.stack(new_k), jnp.stack(new_v), jnp.stack(new_s), jnp.stack(new_shift)


def setup_inputs(seed: int = 0) -> dict:
    key = jax.random.key(seed)
    ks = iter(jax.random.split(key, 40))
    f32 = jnp.float32
    nrm = lambda shape, s=1.0: jax.random.normal(next(ks), shape, f32) * s
    n_pages = PAST_LEN // PAGE_SIZE
    used = DEC_BATCH * n_pages
    n_pool = used + max(used // 4, 1)
    page_table = jax.random.permutation(next(ks), n_pool)[:used].reshape(DEC_BATCH, n_pages).astype(jnp.int32)
    return {
        "x_prompt": nrm((BATCH, SEQ, D_MODEL)),
        "x_sample": nrm((DEC_BATCH, DEC_SEQ, D_MODEL)),
        "cache_k": nrm((DEPTH, n_pool, PAGE_SIZE, DA_HEADS, 2 * DA_QK)),
        "cache_v": nrm((DEPTH, n_pool, PAGE_SIZE, DA_HEADS, DA_V)),
        "page_table": page_table,
        "state_rwkv": nrm((DEPTH, DEC_BATCH, RW_HEADS, RW_HEAD, RW_HEAD), 0.3),
        "state_shift": nrm((DEPTH, DEC_BATCH, D_MODEL)),
        "meta_tokens": nrm((N_META, D_MODEL)),
        "rel_bias": nrm((N_BUCKETS, DA_HEADS), 0.3),
        "norm_g": 1.0 + nrm((DEPTH, D_MODEL), 0.05),
        "w_in": nrm((DEPTH, D_MODEL, N_COLS), D_MODEL ** -0.5),
        "mu_shift": jax.random.uniform(next(ks), (DEPTH, RW_COLS), f32),
        "w0": jax.random.uniform(next(ks), (DEPTH, RW_WIDTH), f32, -5.0, 1.0),
        "w2": nrm((DEPTH, DECAY_LORA, RW_WIDTH), 0.5 * DECAY_LORA ** -0.5),
        "a0": nrm((DEPTH, RW_WIDTH), 0.5),
        "a2": nrm((DEPTH, ICLR_LORA, RW_WIDTH), 0.5 * ICLR_LORA ** -0.5),
        "k_k": 0.85 + nrm((DEPTH, RW_WIDTH), 0.05),
        "k_a": 1.0 + nrm((DEPTH, RW_WIDTH), 0.05),
        "r_k": nrm((DEPTH, RW_HEADS, RW_HEAD), 0.1),
        "lnx_g": 1.0 + nrm((DEPTH, RW_WIDTH), 0.05),
        "lnx_b": nrm((DEPTH, RW_WIDTH), 0.02),
        "q_norm_g": 1.0 + nrm((DEPTH, DA_QK), 0.05),
        "k_norm_g": 1.0 + nrm((DEPTH, DA_QK), 0.05),
        "lam_q1": nrm((DEPTH, DA_QK), 0.1),
        "lam_k1": nrm((DEPTH, DA_QK), 0.1),
        "lam_q2": nrm((DEPTH, DA_QK), 0.1),
        "lam_k2": nrm((DEPTH, DA_QK), 0.1),
        "subln_g": 1.0 + nrm((DEPTH, DA_V), 0.05),
        "w_a_out": nrm((DEPTH, RW_WIDTH, D_MODEL), RW_WIDTH ** -0.5),
        "w_b_out": nrm((DEPTH, DA_HEADS * DA_V, D_MODEL), (DA_HEADS * DA_V) ** -0.5),
        "w_o": nrm((DEPTH, D_MODEL, D_MODEL), D_MODEL ** -0.5),
    }


def reference(x_prompt, x_sample, cache_k, cache_v, page_table, state_rwkv, state_shift,
              meta_tokens, rel_bias, norm_g, w_in, mu_shift, w0, w2, a0, a2, k_k, k_a, r_k,
              lnx_g, lnx_b, q_norm_g, k_norm_g, lam_q1, lam_k1, lam_q2, lam_k2, subln_g,
              w_a_out, w_b_out, w_o):
    B = x_prompt.shape[0]
    meta = jnp.broadcast_to(meta_tokens[None].astype(x_prompt.dtype), (B, N_META, D_MODEL))
    xp = jnp.concatenate([meta, x_prompt], axis=1)
    pos_p = jnp.arange(xp.shape[1])
    yp, kp, vp, sp, hp = trunk(xp, pos_p, None, None, None, None, None, rel_bias, norm_g, w_in,
                               mu_shift, w0, w2, a0, a2, k_k, k_a, r_k, lnx_g, lnx_b, q_norm_g,
                               k_norm_g, lam_q1, lam_k1, lam_q2, lam_k2, subln_g, w_a_out,
                               w_b_out, w_o)
    past = page_table.shape[1] * PAGE_SIZE
    pos_s = past + jnp.arange(x_sample.shape[1])
    ys, k_s, v_s, s_s, h_s = trunk(x_sample, pos_s, state_shift, state_rwkv, cache_k, cache_v,
                                   page_table, rel_bias, norm_g, w_in, mu_shift, w0, w2, a0, a2,
                                   k_k, k_a, r_k, lnx_g, lnx_b, q_norm_g, k_norm_g, lam_q1,
                                   lam_k1, lam_q2, lam_k2, subln_g, w_a_out, w_b_out, w_o)
    return (yp[:, N_META:], ys, kp, vp, sp, hp, k_s, v_s, s_s, h_s)
```

```python
import math
import contextlib
import numpy as np
import concourse.bass as bass
import concourse.mybir as mybir
from concourse.bass_utils import run_bass_kernel_spmd

F32 = mybir.dt.float32
BF16 = mybir.dt.bfloat16
I32 = mybir.dt.int32
AF = mybir.ActivationFunctionType
ALU = mybir.AluOpType
AX = mybir.AxisListType
ENGS = ("pe", "act", "dve", "pool", "sp")


class Op:
    __slots__ = ("eng", "emit", "deps", "inc", "semkey", "count", "is_dma", "waits")

    def __init__(self, eng, emit, is_dma, semkey):
        self.eng = eng; self.emit = emit; self.deps = []; self.inc = False
        self.semkey = semkey; self.count = 0; self.is_dma = is_dma; self.waits = []


class KB:
    def __init__(self, nc):
        self.nc = nc
        self.ops = {e: [] for e in ENGS}
        self.last_w = {}; self.readers = {}; self.dma_cnt = {}

    def add(self, eng, emit, reads=(), writes=(), dma=False, semkey=None):
        if dma and semkey is None:
            semkey = "dma_" + eng
        op = Op(eng, emit, dma, semkey)
        deps = []
        for k in reads:
            w = self.last_w.get(k)
            if w is not None:
                deps.append(w)
        for k in writes:
            w = self.last_w.get(k)
            if w is not None:
                deps.append(w)
            deps.extend(self.readers.get(k, ()))
        seen = set()
        for d in deps:
            if id(d) in seen:
                continue
            seen.add(id(d))
            if d.eng == "pe" and eng == "pe" and not d.is_dma and not dma:
                continue
            if d.is_dma:
                op.waits.append((d.semkey, self.dma_cnt[d.semkey] * 16))
            else:
                d.inc = True
                op.deps.append(d)
        for k in reads:
            self.readers.setdefault(k, []).append(op)
        for k in writes:
            self.last_w[k] = op
            self.readers[k] = []
        if dma:
            self.dma_cnt[semkey] = self.dma_cnt.get(semkey, 0) + 1
        self.ops[eng].append(op)
        return op

    def emit_all(self):
        nc = self.nc
        for e in ENGS:
            c = 0
            for op in self.ops[e]:
                if not op.is_dma and op.inc:
                    c += 1
                    op.count = c
        semkeys = sorted(self.dma_cnt.keys())
        with contextlib.ExitStack() as st:
            esem = {e: st.enter_context(nc.semaphore("s_" + e)) for e in ENGS}
            dsem = {k: st.enter_context(nc.semaphore("d%d" % i)) for i, k in enumerate(semkeys)}
            block = st.enter_context(nc.Block())
            reg = {"pe": block.tensor, "act": block.scalar, "dve": block.vector,
                   "pool": block.gpsimd, "sp": block.sync}
            for e in ENGS:
                def body(eng, ops=self.ops[e], e=e):
                    seen_e = {x: 0 for x in ENGS}
                    seen_d = {}
                    for op in ops:
                        for (k, v) in op.waits:
                            if v > seen_d.get(k, 0):
                                seen_d[k] = v
                                eng.wait_ge(dsem[k], v)
                        need = {}
                        for d in op.deps:
                            if d.count > seen_e[d.eng]:
                                need[d.eng] = max(need.get(d.eng, 0), d.count)
                        for de, v in need.items():
                            seen_e[de] = v
                            eng.wait_ge(esem[de], v)
                        ins = op.emit(eng)
                        if op.is_dma:
                            ins.then_inc(dsem[op.semkey], 16)
                        elif op.inc:
                            ins.then_inc(esem[e], 1)
                    if e == "sp":
                        for k in semkeys:
                            eng.wait_ge(dsem[k], self.dma_cnt[k] * 16)
                reg[e](body)


D = 1024
NCOLS = 6272
RWC = 2176
C_R, C_K, C_V, C_WD, C_Z = 0, 512, 1024, 1536, 1664
C_Q, C_KK, C_VV, C_ZB, C_GA, C_GB = 2176, 2688, 3200, 3712, 4224, 5248
NEGM = -30000.0
import os
STAGE = int(os.environ.get('KSTAGE', '99'))
KSUB = int(os.environ.get('KSUB', '9'))
KX = int(os.environ.get('KX', '9'))
KC = int(os.environ.get('KC', '9'))
KH = int(os.environ.get('KH', '9'))


def t5_bucket_np(d):
    d = np.maximum(d, 0)
    nf = np.maximum(d, 1).astype(np.float32)
    large = 16 + (np.log(nf / 16) / np.float32(math.log(128 / 16)) * 16).astype(np.int32)
    large = np.minimum(large, 31)
    return np.where(d < 16, d, large)


def make_consts():
    k = np.arange(128)[:, None]
    q = np.arange(128)[None, :]
    mats = np.zeros((5, 128, 128), np.float32)
    msk = np.zeros((5, 128, 128), np.float32)
    d0 = q - k
    mats[0] = np.where(d0 >= 0, t5_bucket_np(d0), -1)
    msk[0] = np.where(d0 >= 0, 0.0, NEGM)
    mats[1] = t5_bucket_np(128 + q - k)
    mats[2] = 31
    tq = q % 8
    mats[3] = t5_bucket_np(128 + tq - k)
    d4 = tq - (k % 8)
    mats[4] = np.where(d4 >= 0, t5_bucket_np(d4), -1)
    msk[4] = np.where(d4 >= 0, 0.0, NEGM)
    return mats, msk


def make_chunk_consts():
    s_ = np.arange(128)[:, None]; t_ = np.arange(128)[None, :]
    cmk = np.zeros((8, 128, 128), np.float32)
    for i, cs in enumerate((16, 8)):
        same = (s_ // cs) == (t_ // cs)
        cmk[0 + i] = same & (s_ <= t_)
        cmk[2 + i] = same & (s_ < t_)
        cmk[4 + i] = same & (t_ < s_)
        cmk[6 + i] = same
    cmc = np.zeros((3, 128, 8), np.float32)
    r = np.arange(128)
    for c in range(8):
        cmc[0, :, c] = (r // 16) == c
        cmc[1, :, c] = (r // 8) == c
        cmc[2, :, c] = (r // 8) == 8 + c
    return np.ascontiguousarray(cmk.transpose(1, 0, 2)), np.ascontiguousarray(cmc.transpose(1, 0, 2))


def build(NPT, NPG, NPOOL):
    NT = NPT + 1
    nc = bass.Bass("TRN2", target_bir_lowering=False)
    di = lambda n, s, d=F32: nc.dram_tensor(n, s, d, kind="ExternalInput").ap()
    do = lambda n, s, d=F32: nc.dram_tensor(n, s, d, kind="ExternalOutput").ap()
    xin = di("xin", [NT * 128, D])
    ck = di("ck", [2 * NPOOL * 128, 512]); cv = di("cv", [2 * NPOOL * 128, 512])
    pt = di("pt", [1, 16 * NPG], I32)
    srw = di("srw", [2, 16, 8, 64, 64]); ssh = di("ssh", [2, 16, D])
    relb = di("relb", [1, 128])
    w_in = di("w_in", [2, D, NCOLS]); w_a = di("w_a", [2, 512, D]); w_b = di("w_b", [2, 512, D]); w_o = di("w_o", [2, D, D])
    w2 = di("w2", [2, 64, 512]); a2 = di("a2", [2, 64, 512])
    vec = di("vec", [2, 1, 16 * 512])
    lamv = di("lamv", [2, 1, 256])
    bm_in = di("bm_in", [128, 5, 128]); mk_in = di("mk_in", [128, 5, 128]); id_in = di("id_in", [128, 128])
    cmk_in = di("cmk_in", [128, 8, 128]); cmc_in = di("cmc_in", [128, 3, 8])
    y_o = do("y_o", [NT * 128, D])
    k_o = do("k_o", [2, NT * 128, 512]); v_o = do("v_o", [2, NT * 128, 512])
    sp_o = do("sp_o", [2, 8, 64, 64]); hp_o = do("hp_o", [2, 1, D])
    ss_o = do("ss_o", [2, 16, 8, 64, 64]); hs_o = do("hs_o", [2, 16, D])
    xs = nc.dram_tensor("xs", [NT * 128, D], F32, kind="Internal").ap()
    wq_in = nc.dram_tensor("wq_in", [2, D, NCOLS], BF16, kind="Internal").ap(); wq_a = nc.dram_tensor("wq_a", [2, 512, D], BF16, kind="Internal").ap()
    wq_b = nc.dram_tensor("wq_b", [2, 512, D], BF16, kind="Internal").ap(); wq_o = nc.dram_tensor("wq_o", [2, D, D], BF16, kind="Internal").ap()

    kb = KB(nc)
    A = lambda eng, fn, r=(), w=(): kb.add(eng, fn, reads=r, writes=w)
    Dm = lambda eng, out, in_, r, w, key: kb.add(eng, lambda e: e.dma_start(out=out, in_=in_), reads=r, writes=w, dma=True, semkey=key)

    with contextlib.ExitStack() as st:
        sb = lambda n, s, d=F32: st.enter_context(nc.sbuf_tensor(n, s, d))
        pst = lambda n: st.enter_context(nc.psum_tensor(n, [128, 512], F32))
        PS = [pst("ps%d" % i) for i in range(8)]
        ident = sb("ident", [128, 128]); identb = sb("identb", [128, 128], BF16)
        rbB = sb("rbB", [128, 128])
        bias = sb("bias", [128, 4, 5, 128], BF16)
        vecB = sb("vecB", [128, 15 * 512])
        lamB = sb("lamB", [128, 256]); lamT = sb("lamT", [128, 4]); lam = sb("lam", [128, 1]); lamn = sb("lamn", [128, 1])
        xt = sb("xt", [128, D]); xn = sb("xn", [128, D]); xsft = sb("xsft", [128, D]); junk = sb("junk", [128, D])
        ssq = sb("ssq", [128, 1]); rstd = sb("rstd", [128, 1])
        xnT = sb("xnT", [128, 8, 128], BF16); xsT = sb("xsT", [128, 8, 128], BF16)
        wbfs = [sb("wbf%d" % i, [128, 8, 256], BF16) for i in range(3)]
        hrw = sb("hrw", [128, RWC]); prj = sb("prj", [128, NCOLS - RWC])
        GT = prj[0:64, 0:2048].bitcast(BF16).rearrange("p (h c j) -> p h c j", h=8, c=8); HS = prj[0:64, 2048:4096].bitcast(BF16).rearrange("p (h c j) -> p h c j", h=8, c=8)
        bm = prj[:, 0:640].rearrange("p (a b) -> p a b", a=5); mk = prj[:, 640:1280].rearrange("p (a b) -> p a b", a=5); eqt = prj[:, 1280:1920].rearrange("p (a b) -> p a b", a=5)
        w2a2 = sb("w2a2", [128, 512]); w2a2b = sb("w2a2b", [128, 512], BF16)
        wdT = sb("wdT", [128, 128], BF16)
        wt = sb("wt", [128, 512]); at = sb("at", [128, 512]); kkt = sb("kkt", [128, 512]); bt = sb("bt", [128, 512]); kmt = sb("kmt", [128, 512])
        t512 = sb("t512", [128, 512]); s8 = sb("s8", [128, 8]); s8b = sb("s8b", [128, 8])
        cmk = sb("cmk", [128, 8, 128]); cmc = sb("cmc", [128, 3, 8])
        TB = sb("TB", [128, 5, 512], BF16)
        Vb = TB[:, 0, :]; Rpb = TB[:, 1, :]; Pmb = TB[:, 2, :]; BCb = TB[:, 3, :]; KCb = TB[:, 4, :]
        kpgB = TB[:, 0:2, :].rearrange("p a c -> p (a c)").bitcast(F32); vpgB = TB[:, 2:4, :].rearrange("p a c -> p (a c)").bitcast(F32)
        kpTB = TB[:, 4, :].rearrange("p (h t) -> p h t", h=4)
        RpT = sb("RpT", [64, 8, 128], BF16); PmT = sb("PmT", [64, 8, 128], BF16); BnT = sb("BnT", [64, 8, 128], BF16); KnT = sb("KnT", [64, 8, 128], BF16)
        GamT = sb("GamT", [64, 8, 16]); MA = sb("MA", [128, 4, 128], BF16); DB = sb("DB", [128, 9, 128], BF16); RR = sb("RR", [128, 4, 128], BF16)
        PUn = sb("PUn", [128, 8, 128], BF16);
        vpxB = PUn[:].rearrange("p (h a) t -> p h (a t)", a=2)[:, :, 0:132]
        RhT = sb("RhT", [64, 8, 128], BF16); Mst = sb("Mst", [64, 8, 64]); Mb = sb("Mb", [64, 8, 64], BF16); Sv = sb("Sv", [64, 8, 64])
        ya = sb("ya", [128, 512]); yb = sb("yb", [128, 512])
        KT = sb("KT", [128, 4, NPT * 128], BF16); Vx = sb("Vx", [128, NPT, 4, 132], BF16)
        qT = sb("qT", [128, 4, 128], BF16)
        kpT = sb("kpT", [128, 4, 128], BF16); vpx = sb("vpx", [128, 4, 132], BF16)
        ksT = sb("ksT", [128, 4, 128], BF16); vsx = sb("vsx", [128, 4, 132], BF16); vs8 = sb("vs8", [8, 4, 132], BF16)
        sc = sb("sc", [128, 8, 128]); pT = sb("pT", [128, 8, 128], BF16)
        _pv = lambda a: pT[:, a:a + 4, :].rearrange("p a (b j) -> p (a b) j", b=2)
        Vm = _pv(0); UVm = _pv(4); BCm = sb("BCm", [128, 8, 64], BF16)
        cwt = sc[:, 0:4, :].rearrange("p a t -> p (a t)")
        osb = sb("osb", [128, 8, 132]); rl = sb("rl", [128, 8]); o2 = sb("o2", [128, 512])
        YL = osb[0:64, :, 0:128]
        yT = xsT
        mm = sb("mm", [128, D]); mT = xsT
        ptb = sb("ptb", [128, 16 * NPG], I32); io = sb("io", [128, 1], I32); pidx = sb("pidx", [128, 16 * NPG], I32); pidx1 = sb("pidx1", [128, 16 * NPG], I32)

        Dm("sp", ident[:], id_in, (), ("ident",), "c0")
        Dm("sp", bm[:], bm_in, (), ("bm", "prj"), "c0")
        Dm("sp", mk[:], mk_in, (), ("mk", "prj"), "c0")
        Dm("sp", rbB[:], relb.partition_broadcast(128), (), ("rbB",), "c0")
        Dm("sp", cmk[:], cmk_in, (), ("cmk",), "c0")
        Dm("sp", cmc[:], cmc_in, (), ("cmc",), "c0")
        Dm("sp", ptb[:], pt.partition_broadcast(128), (), ("ptb",), "c0")
        A("dve", lambda e: e.tensor_copy(out=identb[:], in_=ident[:]), ("ident",), ("identb",))
        A("pool", lambda e: e.iota(io[:], pattern=[[0, 1]], base=0, channel_multiplier=1), (), ("io",))
        A("dve", lambda e: e.tensor_scalar(out=pidx[:], in0=ptb[:], scalar1=128.0, scalar2=io[:, 0:1], op0=ALU.mult, op1=ALU.add), ("ptb", "io"), ("pidx",))
        A("dve", lambda e: e.tensor_single_scalar(out=pidx1[:], in_=pidx[:], scalar=float(NPOOL * 128), op=ALU.add), ("pidx",), ("pidx",))
        for h in range(4):
            A("dve", lambda e, h=h: e.tensor_copy(out=bias[:, h, :, :], in_=mk[:]), ("mk", "prj"), ("bias",))
        for b in range(32):
            A("dve", lambda e, b=b: e.tensor_single_scalar(out=eqt[:], in_=bm[:], scalar=float(b), op=ALU.is_equal), ("bm", "prj"), ("eqt", "prj"))
            for h in range(4):
                A("dve", lambda e, b=b, h=h: e.scalar_tensor_tensor(out=bias[:, h, :, :], in0=eqt[:], scalar=rbB[:, b * 4 + h:b * 4 + h + 1], in1=bias[:, h, :, :], op0=ALU.mult, op1=ALU.add), ("eqt", "prj", "rbB", "bias"), ("bias",))
        for ti in range(NT):
            Dm("sp", xt[:], xin[ti * 128:(ti + 1) * 128, :], (), ("xt",), "xin")
            Dm("sp", xs[ti * 128:(ti + 1) * 128, :], xt[:], ("xt",), ("xs%d" % ti,), "xso")
        A("pool", lambda e: e.memset(Vx[:], 0.0), (), ("Vx",))
        A("pool", lambda e: e.memset(Vx[:, :, :, 128:129], 1.0), ("Vx",), ("Vx",))
        A("pool", lambda e: e.memset(vpx[:], 1.0), (), ("vpx",))
        A("pool", lambda e: e.memset(vsx[:], 1.0), (), ("vsx",))

        tr_ctr = [0]

        def transpose_to(dst_fn, src_ap_fn, n, rkeys, wkey, ps_i=5, dst4=None):
            for j0 in range(0, n, 4):
                nn = min(4, n - j0)
                pb = ps_i
                if ps_i == 5:
                    pb = 5 + (tr_ctr[0] % 2)
                    tr_ctr[0] += 1
                for j in range(nn):
                    A("pe", lambda e, j=j, j0=j0, pb=pb: e.transpose(PS[pb][:, j * 128:(j + 1) * 128], src_ap_fn(j0 + j), ident[:]), tuple(rkeys) + ("ident",), ("ps%d" % pb,))
                if dst4 is not None:
                    A("act", lambda e, j0=j0, nn=nn, pb=pb: e.activation(out=dst4(j0, nn), in_=PS[pb][:, 0:nn * 128].rearrange("p (a t) -> p a t", a=nn), func=AF.Copy), ("ps%d" % pb,), (wkey,))
                else:
                    for j in range(nn):
                        A("act", lambda e, j=j, j0=j0, pb=pb: e.activation(out=dst_fn(j0 + j), in_=PS[pb][:, j * 128:(j + 1) * 128], func=AF.Copy), ("ps%d" % pb,), (wkey,))

        slot_ctr = [0]

        def linear(dst, lhsT_fn, nk, wsrc, c0, ncols, rkeys, wkey, pname, evac_eng="act"):
            wq, wqk = wsrc
            for g0 in range(0, ncols, 256):
                gw = min(256, ncols - g0)
                sl = slot_ctr[0] % 3
                pb = 6 + (slot_ctr[0] % 2)
                slot_ctr[0] += 1
                wt_ = wbfs[sl]
                Dm("sp", wt_[:, 0:nk, 0:gw], wq[:, c0 + g0:c0 + g0 + gw].rearrange("(k p) c -> p k c", p=128), (wqk,), ("wbf%d" % sl,), "wl%d" % sl)
                for k in range(nk):
                    A("pe", lambda e, k=k, gw=gw, wt_=wt_, pb=pb: e.matmul(PS[pb][:, 0:gw], lhsT=lhsT_fn(k), rhs=wt_[:, k, 0:gw], start=(k == 0), stop=(k == nk - 1)), tuple(rkeys) + ("wbf%d" % sl,), ("ps%d" % pb,))
                A(evac_eng, (lambda e, g0=g0, gw=gw, pb=pb: e.activation(out=dst[:, g0:g0 + gw], in_=PS[pb][:, 0:gw], func=AF.Copy)), ("ps%d" % pb,), (wkey,))

        stg = hrw[:, 0:2048].rearrange("p (k c) -> p k c", k=8)
        cvt = 0
        for l in range(2):
            for (src, dstq, nk, ncols, qk) in ((w_in[l], wq_in[l], 8, NCOLS, "wq_in%d" % l), (w_a[l], wq_a[l], 4, D, "wq_a%d" % l), (w_b[l], wq_b[l], 4, D, "wq_b%d" % l), (w_o[l], wq_o[l], 8, D, "wq_o%d" % l)):
                for g0 in range(0, ncols, 256):
                    gw = min(256, ncols - g0)
                    sl = cvt % 3
                    ce = ("act", "dve", "pool")[cvt % 3]
                    cvt += 1
                    Dm("sp", stg[:, 0:nk, 0:gw], src[:, g0:g0 + gw].rearrange("(k p) c -> p k c", p=128), (), ("hrw",), "wst")
                    if ce == "act":
                        A("act", lambda e, sl=sl, nk=nk, gw=gw: e.activation(out=wbfs[sl][:, 0:nk, 0:gw], in_=stg[:, 0:nk, 0:gw], func=AF.Copy), ("hrw",), ("wbf%d" % sl,))
                    else:
                        A(ce, lambda e, sl=sl, nk=nk, gw=gw: e.tensor_copy(out=wbfs[sl][:, 0:nk, 0:gw], in_=stg[:, 0:nk, 0:gw]), ("hrw",), ("wbf%d" % sl,))
                    Dm("sp", dstq[:, g0:g0 + gw].rearrange("(k p) c -> p k c", p=128), wbfs[sl][:, 0:nk, 0:gw], ("wbf%d" % sl,), (qk,), "wqo")

        VEC = {n: i for i, n in enumerate(["norm_g", "norm_g2", "mu0", "mu1", "mu2", "mu3", "mu4", "w0", "a0", "k_k", "k_a", "r_k", "lnx_g", "lnx_b", "misc", "pad"])}
        vv = lambda name, n=512: vecB[:, VEC[name] * 512:VEC[name] * 512 + n]
        SCALE = 0.125

        def rsqrt_ip(ap_fn, key):
            A("act", lambda e: e.activation(out=ap_fn(), in_=ap_fn(), func=AF.Sqrt), (key,), (key,))
            A("dve", lambda e: e.reciprocal(out=ap_fn(), in_=ap_fn()), (key,), (key,))

        for l in range(2):
            li = 0.8 - 0.6 * math.exp(-0.3 * l)
            Dm("sp", vecB[:], vec[l][:, 0:15 * 512].partition_broadcast(128), (), ("vecB",), "c1")
            Dm("sp", lamB[:], lamv[l].partition_broadcast(128), (), ("lamB",), "c1")
            Dm("sp", w2a2[0:64, :], w2[l], (), ("w2a2",), "c1")
            Dm("sp", w2a2[64:128, :], a2[l], (), ("w2a2",), "c1")
            A("dve", lambda e: e.tensor_copy(out=w2a2b[:], in_=w2a2[:]), ("w2a2",), ("w2a2b",))
            A("dve", lambda e: e.tensor_tensor(out=lamB[:, 0:128], in0=lamB[:, 0:128], in1=lamB[:, 128:256], op=ALU.mult), ("lamB",), ("lamB",))
            A("dve", lambda e: e.tensor_reduce(out=lamT[:, 0:2], in_=lamB[:, 0:128].rearrange("p (a b) -> p a b", a=2), axis=AX.X, op=ALU.add), ("lamB",), ("lamT",))
            A("act", lambda e: e.activation(out=lamT[:, 2:4], in_=lamT[:, 0:2], func=AF.Exp), ("lamT",), ("lamT",))
            A("dve", lambda e: e.tensor_tensor(out=lam[:], in0=lamT[:, 2:3], in1=lamT[:, 3:4], op=ALU.subtract), ("lamT",), ("lam",))
            A("dve", lambda e, li=li: e.tensor_scalar(out=lamn[:], in0=lam[:], scalar1=li, scalar2=-1.0, op0=ALU.add, op1=ALU.mult), ("lam",), ("lamn",))
            A("pool", lambda e: e.memset(Mst[:], 0.0), (), ("Mst",))
            A("pool", lambda e: e.memset(Mb[:], 0.0), (), ("Mb",))
            A("pool", lambda e: e.memset(xsft[:], 0.0), (), ("xsft",))

            def tile_body(l, ti, li):
                samp = (ti == NT - 1)
                xk = "xs%d" % ti
                Dm("sp", xt[:], xs[ti * 128:(ti + 1) * 128, :], (xk,), ("xt",), "xin")
                A("act", lambda e: e.activation(out=junk[:], in_=xt[:], func=AF.Square, accum_out=ssq[:]), ("xt",), ("junk", "ssq"))
                A("dve", lambda e: e.tensor_scalar(out=rstd[:], in0=ssq[:], scalar1=1.0 / D, scalar2=1e-6, op0=ALU.mult, op1=ALU.add), ("ssq",), ("rstd",))
                rsqrt_ip(lambda: rstd[:], "rstd")
                if not samp:
                    pass
                A("dve", lambda e: e.scalar_tensor_tensor(out=xn[:, 0:512], in0=xt[:, 0:512], scalar=rstd[:, 0:1], in1=vecB[:, 0:512], op0=ALU.mult, op1=ALU.mult), ("xt", "rstd", "vecB"), ("xn",))
                A("dve", lambda e: e.scalar_tensor_tensor(out=xn[:, 512:1024], in0=xt[:, 512:1024], scalar=rstd[:, 0:1], in1=vecB[:, 512:1024], op0=ALU.mult, op1=ALU.mult), ("xt", "rstd", "vecB"), ("xn",))
                if not samp:
                    Dm("sp", xsft[1:128, :], xn[0:127, :], ("xn",), ("xsft",), "sft")
                    if ti == NPT - 1:
                        Dm("sp", hp_o[l], xn[127:128, :], ("xn",), (), "out")
                else:
                    Dm("sp", xsft[1:128, :], xn[0:127, :], ("xn",), ("xsft",), "sft")
                    Dm("sp", xsft[0:128:8, :], ssh[l], ("xsft",), ("xsft",), "sft")
                    Dm("sp", hs_o[l], xn[7:128:8, :], ("xn",), (), "out")
                transpose_to(lambda j: xnT[:, j, :], lambda j: xn[:, j * 128:(j + 1) * 128], 8, ("xn",), "xnT", dst4=lambda j0, nn: xnT[:, j0:j0 + nn, :])
                transpose_to(lambda j: xsT[:, j, :], lambda j: xsft[:, j * 128:(j + 1) * 128], 8, ("xsft",), "xsT", dst4=lambda j0, nn: xsT[:, j0:j0 + nn, :])
                if not samp:
                    Dm("sp", xsft[0:1, :], xn[127:128, :], ("xn", "xsT"), ("xsft",), "sft")
                if STAGE <= 0:
                    return
                linear(hrw, lambda k: xnT[:, k, :], 8, (wq_in[l], "wq_in%d" % l), 0, RWC, ("xnT",), "hrw", "p")
                for c0 in range(0, RWC, 1024):
                    cw = min(1024, RWC - c0)
                    linear(mm, lambda k: xsT[:, k, :], 8, (wq_in[l], "wq_in%d" % l), c0, cw, ("xsT",), "mm", "p")
                    A("dve", lambda e, c0=c0, cw=cw: e.tensor_tensor(out=mm[:, 0:cw], in0=mm[:, 0:cw], in1=hrw[:, c0:c0 + cw], op=ALU.subtract), ("mm", "hrw"), ("mm",))
                    A("dve", lambda e, c0=c0, cw=cw: e.tensor_tensor(out=mm[:, 0:cw], in0=mm[:, 0:cw], in1=vecB[:, 1024 + c0:1024 + c0 + cw], op=ALU.mult), ("mm", "vecB"), ("mm",))
                    A("dve", lambda e, c0=c0, cw=cw: e.tensor_tensor(out=hrw[:, c0:c0 + cw], in0=hrw[:, c0:c0 + cw], in1=mm[:, 0:cw], op=ALU.add), ("mm", "hrw"), ("hrw",))
                PQ = lambda c, n: prj[:, c - RWC:c - RWC + n]
                if STAGE <= 1:
                    return
                A("pe", lambda e: e.transpose(PS[5][:, 0:128], hrw[:, C_WD:C_WD + 128], ident[:]), ("hrw", "ident"), ("ps5",))
                A("act", lambda e: e.activation(out=wdT[0:64, :], in_=PS[5][0:64, 0:128], func=AF.Tanh), ("ps5",), ("wdT",))
                A("act", lambda e: e.activation(out=wdT[64:128, :], in_=PS[5][64:128, 0:128], func=AF.Copy), ("ps5",), ("wdT",))
                A("pe", lambda e: e.matmul(PS[5][:, :], lhsT=wdT[0:64, :], rhs=w2a2b[0:64, :], start=True, stop=True), ("wdT", "w2a2b"), ("ps5",))
                A("pe", lambda e: e.matmul(PS[7][:, :], lhsT=wdT[64:128, :], rhs=w2a2b[64:128, :], start=True, stop=True), ("wdT", "w2a2b"), ("ps7",))
                A("dve", lambda e: e.tensor_tensor(out=wt[:], in0=PS[5][:, :], in1=vv("w0"), op=ALU.add), ("ps5", "vecB"), ("wt",))
                A("dve", lambda e: e.tensor_tensor(out=at[:], in0=PS[7][:, :], in1=vv("a0"), op=ALU.add), ("ps7", "vecB"), ("at",))
                A("act", lambda e: e.activation(out=wt[:], in_=wt[:], func=AF.Sigmoid), ("wt",), ("wt",))
                A("act", lambda e: e.activation(out=at[:], in_=at[:], func=AF.Sigmoid), ("at",), ("at",))
                A("dve", lambda e: e.tensor_single_scalar(out=wt[:], in_=wt[:], scalar=-math.exp(-0.5), op=ALU.mult), ("wt",), ("wt",))
                A("dve", lambda e: e.tensor_tensor(out=kkt[:], in0=hrw[:, C_K:C_K + 512], in1=vv("k_k"), op=ALU.mult), ("hrw", "vecB"), ("kkt",))
                A("dve", lambda e: e.tensor_tensor(out=t512[:], in0=kkt[:], in1=kkt[:], op=ALU.mult), ("kkt",), ("t512",))
                A("dve", lambda e: e.tensor_reduce(out=s8[:], in_=t512[:].rearrange("p (h j) -> p h j", h=8), axis=AX.X, op=ALU.add), ("t512",), ("s8",))
                A("dve", lambda e: e.tensor_single_scalar(out=s8[:], in_=s8[:], scalar=1e-24, op=ALU.max), ("s8",), ("s8",))
                rsqrt_ip(lambda: s8[:], "s8")
                A("dve", lambda e: e.tensor_tensor(out=kkt[:].rearrange("p (h j) -> p h j", h=8), in0=kkt[:].rearrange("p (h j) -> p h j", h=8), in1=s8[:].unsqueeze(2).to_broadcast([128, 8, 64]), op=ALU.mult), ("kkt", "s8"), ("kkt",))
                A("dve", lambda e: e.tensor_tensor(out=bt[:], in0=kkt[:], in1=at[:], op=ALU.mult), ("kkt", "at"), ("bt",))
                A("dve", lambda e: e.scalar_tensor_tensor(out=t512[:], in0=at[:], scalar=-1.0, in1=vv("k_a"), op0=ALU.add, op1=ALU.mult), ("at", "vecB"), ("t512",))
                A("dve", lambda e: e.scalar_tensor_tensor(out=kmt[:], in0=t512[:], scalar=1.0, in1=hrw[:, C_K:C_K + 512], op0=ALU.add, op1=ALU.mult), ("t512", "hrw"), ("kmt",))
                if STAGE <= 2:
                    return
                CSi = 1 if samp else 0
                CS = 8 if samp else 16
                NCHT = 128 // CS
                TRIi = cmk[:, CSi, :]; TRIs = cmk[:, 2 + CSi, :]; LOWs = cmk[:, 4 + CSi, :]; BLK = cmk[:, 6 + CSi, :]
                x32 = junk[:, 0:512]
                A("pe", lambda e: e.matmul(PS[0][:, :], lhsT=TRIi, rhs=wt[:], start=True, stop=True), ("cmk", "wt"), ("ps0",))
                A("pe", lambda e: e.matmul(PS[1][:, :], lhsT=BLK, rhs=wt[:], start=True, stop=True), ("cmk", "wt"), ("ps1",))
                A("act", lambda e: e.activation(out=cwt[:], in_=PS[0][:, :], func=AF.Copy), ("ps0",), ("sc",))
                A("pool", lambda e: e.tensor_copy(out=Vb[:], in_=hrw[:, C_V:C_V + 512]), ("hrw",), ("Vb",))

                if KC <= 1:
                    return
                def tr8(dstT):
                    for g in range(2):
                        for k in range(4):
                            A("pe", lambda e, g=g, k=k: e.transpose(PS[5 + g][0:64, k * 128:(k + 1) * 128], junk[:, (g * 4 + k) * 64:(g * 4 + k + 1) * 64], ident[:]), ("junk", "ident"), ("ps%d" % (5 + g),))
                        A("act", lambda e, g=g: e.activation(out=dstT[:, g * 4:(g + 1) * 4, :], in_=PS[5 + g][0:64, :].rearrange("p (k t) -> p k t", k=4), func=AF.Copy), ("ps%d" % (5 + g),), ("xT",))
                A("act", lambda e: e.activation(out=t512[:], in_=PS[0][:, :], func=AF.Exp), ("ps0",), ("t512",))
                A("dve", lambda e: e.tensor_tensor(out=x32, in0=hrw[:, C_R:C_R + 512], in1=t512[:], op=ALU.mult), ("hrw", "t512"), ("junk",))
                A("pool", lambda e: e.tensor_copy(out=Rpb[:], in_=x32), ("junk",), ("Rpb",))
                tr8(RpT)
                A("dve", lambda e: e.tensor_tensor(out=t512[:], in0=cwt[:], in1=wt[:], op=ALU.subtract), ("sc", "wt"), ("t512",))
                A("act", lambda e: e.activation(out=t512[:], in_=t512[:], func=AF.Exp), ("t512",), ("t512",))
                A("dve", lambda e: e.tensor_tensor(out=x32, in0=kkt[:], in1=t512[:], op=ALU.mult), ("kkt", "t512"), ("junk",))
                A("pool", lambda e: e.tensor_copy(out=Pmb[:], in_=x32), ("junk",), ("Pmb",))
                tr8(PmT)
                A("act", lambda e: e.activation(out=t512[:], in_=cwt[:], func=AF.Exp, scale=-1.0), ("sc",), ("t512",))
                A("dve", lambda e: e.tensor_tensor(out=x32, in0=bt[:], in1=t512[:], op=ALU.mult), ("bt", "t512"), ("junk",))
                tr8(BnT)
                A("dve", lambda e: e.tensor_tensor(out=x32, in0=kmt[:], in1=t512[:], op=ALU.mult), ("kmt", "t512"), ("junk",))
                tr8(KnT)
                A("dve", lambda e: e.tensor_tensor(out=t512[:], in0=PS[1][:, :], in1=cwt[:], op=ALU.subtract), ("ps1", "sc"), ("t512",))
                A("act", lambda e: e.activation(out=t512[:], in_=t512[:], func=AF.Exp), ("t512",), ("t512",))
                A("dve", lambda e: e.tensor_tensor(out=BCb[:], in0=bt[:], in1=t512[:], op=ALU.mult), ("bt", "t512"), ("BCb",))
                A("dve", lambda e: e.tensor_tensor(out=KCb[:], in0=kmt[:], in1=t512[:], op=ALU.mult), ("kmt", "t512"), ("KCb",))
                A("act", lambda e: e.activation(out=x32, in_=PS[1][:, :], func=AF.Exp), ("ps1", "junk"), ("junk",))
                for g in range(2):
                    for k in range(4):
                        A("pe", lambda e, g=g, k=k: e.transpose(PS[5 + g][0:64, k * 128:(k + 1) * 128], junk[:, (g * 4 + k) * 64:(g * 4 + k + 1) * 64], ident[:]), ("junk", "ident"), ("ps%d" % (5 + g),))
                    A("act", lambda e, g=g: e.activation(out=GamT[:, g * 4:(g + 1) * 4, 0:NCHT], in_=PS[5 + g][0:64, :].rearrange("p (k t) -> p k t", k=4)[:, :, 0:128:CS], func=AF.Copy), ("ps%d" % (5 + g),), ("GamT",))
                if KC <= 2:
                    return
                hs = lambda h: slice(h * 64, (h + 1) * 64)
                for h in range(8):
                    XK = ("xT",)
                    A("pe", lambda e, h=h: e.matmul(PS[2][:, 0:128], lhsT=BnT[:, h, :], rhs=PmT[:, h, :], start=True, stop=True), XK, ("ps2",))
                    A("pe", lambda e, h=h: e.matmul(PS[2][:, 128:256], lhsT=BnT[:, h, :], rhs=RpT[:, h, :], start=True, stop=True), XK, ("ps2",))
                    A("pe", lambda e, h=h: e.matmul(PS[2][:, 256:384], lhsT=KnT[:, h, :], rhs=PmT[:, h, :], start=True, stop=True), XK, ("ps2",))
                    A("pe", lambda e, h=h: e.matmul(PS[2][:, 384:512], lhsT=KnT[:, h, :], rhs=RpT[:, h, :], start=True, stop=True), XK, ("ps2",))
                    A("pe", lambda e, h=h: e.matmul(PS[3][:, 0:128], lhsT=PmT[:, h, :], rhs=BnT[:, h, :], start=True, stop=True), XK, ("ps3",))
                    if KH <= 1:
                        continue
                    P2 = lambda: PS[2][:, :].rearrange("p (a t) -> p a t", a=4)
                    A("dve", lambda e: e.tensor_tensor(out=MA[:, 0:4:2, :], in0=P2()[:, 0:4:2, :], in1=TRIs.unsqueeze(1).to_broadcast([128, 2, 128]), op=ALU.mult), ("ps2", "cmk"), ("MA",))
                    A("dve", lambda e: e.tensor_tensor(out=MA[:, 1:4:2, :], in0=P2()[:, 1:4:2, :], in1=TRIi.unsqueeze(1).to_broadcast([128, 2, 128]), op=ALU.mult), ("ps2", "cmk"), ("MA",))
                    A("dve", lambda e: e.tensor_tensor(out=DB[:, 0, :], in0=PS[3][:, 0:128], in1=LOWs, op=ALU.mult), ("ps3", "cmk"), ("DB0",))
                    A("pool", lambda e: e.tensor_tensor(out=DB[:, 1, :], in0=identb[:], in1=MA[:, 0, :], op=ALU.subtract), ("identb", "MA"), ("DB1",))
                    if KH <= 2:
                        continue
                    A("pe", lambda e, h=h: e.matmul(PS[3][:, 128:192], lhsT=MA[:, 2, :], rhs=Vb[:, hs(h)], start=True, stop=True), ("MA", "Vb"), ("ps3",))
                    A("act", lambda e: e.activation(out=RR[:, 0, 64:128], in_=PS[3][:, 128:192], func=AF.Copy), ("ps3",), ("RR0",))
                    A("pool", lambda e, h=h: e.tensor_copy(out=RR[:, 0, 0:64], in_=Pmb[:, hs(h)]), ("Pmb", "RR0"), ("RR0",))
                    if KH <= 3:
                        continue
                    A("pe", lambda e: e.matmul(PS[4][:, 0:128], lhsT=DB[:, 0, :], rhs=MA[:, 0, :], start=True, stop=True), ("DB0", "MA"), ("ps4",))
                    A("pe", lambda e: e.matmul(PS[4][:, 128:256], lhsT=MA[:, 0, :], rhs=DB[:, 0, :], start=True, stop=True), ("DB0", "MA"), ("ps4",))
                    A("act", lambda e: e.activation(out=DB[:, 2:4, :], in_=PS[4][:, 0:256].rearrange("p (a t) -> p a t", a=2), func=AF.Copy), ("ps4",), ("DB23",))
                    A("pool", lambda e: e.tensor_tensor(out=DB[:, 4, :], in0=DB[:, 2, :], in1=identb[:], op=ALU.add), ("DB23", "identb"), ("DB4",))
                    A("pe", lambda e: e.matmul(PS[4][:, 256:384], lhsT=DB[:, 3, :], rhs=DB[:, 2, :], start=True, stop=True), ("DB23",), ("ps4",))
                    A("pe", lambda e: e.matmul(PS[4][:, 384:512], lhsT=DB[:, 2, :], rhs=DB[:, 3, :], start=True, stop=True), ("DB23",), ("ps4",))
                    A("act", lambda e: e.activation(out=DB[:, 5:7, :], in_=PS[4][:, 256:512].rearrange("p (a t) -> p a t", a=2), func=AF.Copy), ("ps4",), ("DB56",))
                    A("pool", lambda e: e.tensor_tensor(out=DB[:, 7, :], in0=DB[:, 5, :], in1=identb[:], op=ALU.add), ("DB56", "identb"), ("DB7",))
                    if CS == 16:
                        A("pe", lambda e: e.matmul(PS[3][:, 256:384], lhsT=DB[:, 6, :], rhs=DB[:, 5, :], start=True, stop=True), ("DB56",), ("ps3",))
                        A("dve", lambda e: e.tensor_tensor(out=DB[:, 8, :], in0=PS[3][:, 256:384], in1=ident[:], op=ALU.add), ("ps3", "ident"), ("DB8",))
                    if KH <= 4:
                        continue
                    lv = [(1, "DB1"), (4, "DB4"), (7, "DB7")] + ([(8, "DB8")] if CS == 16 else [])
                    for k, (di, dk) in enumerate(lv):
                        A("pe", lambda e, k=k, di=di: e.matmul(PS[7][:, k * 128:(k + 1) * 128], lhsT=DB[:, di, :], rhs=RR[:, k, :], start=True, stop=True), (dk, "RR%d" % k), ("ps7",))
                        if k < len(lv) - 1:
                            A("act", lambda e, k=k: e.activation(out=RR[:, k + 1, :], in_=PS[7][:, k * 128:(k + 1) * 128], func=AF.Copy), ("ps7",), ("RR%d" % (k + 1),))
                        else:
                            A("act", lambda e, k=k, h=h: e.activation(out=PUn[:, h, :], in_=PS[7][:, k * 128:(k + 1) * 128], func=AF.Copy, scale=-1.0), ("ps7",), ("PUn",))
                    if KH <= 5:
                        continue
                    A("pe", lambda e, h=h: e.matmul(PS[3][0:64, 384:512], lhsT=Rpb[:, hs(h)], rhs=identb[:], start=True, stop=False), ("Rpb", "identb"), ("ps3",))
                    A("pe", lambda e, h=h: e.matmul(PS[3][0:64, 384:512], lhsT=PUn[:, h, 0:64], rhs=MA[:, 1, :], start=False, stop=True), ("PUn", "MA"), ("ps3",))
                    A("act", lambda e, h=h: e.activation(out=RhT[:, h, :], in_=PS[3][0:64, 384:512], func=AF.Copy), ("ps3",), ("RhT",))
                    if KH <= 6:
                        continue
                    A("pe", lambda e, h=h: e.matmul(PS[5][0:64, 0:128], lhsT=Vb[:, hs(h)], rhs=MA[:, 3, :], start=True, stop=False), ("Vb", "MA"), ("ps5",))
                    A("pe", lambda e, h=h: e.matmul(PS[5][0:64, 0:128], lhsT=PUn[:, h, 64:128], rhs=MA[:, 1, :], start=False, stop=True), ("PUn", "MA"), ("ps5",))
                    A("act", lambda e, h=h: e.activation(out=YL[:, h, :], in_=PS[5][0:64, 0:128], func=AF.Copy), ("ps5",), ("osb",))
                if KC <= 3:
                    return
                YTP = lambda h: PS[h // 4][0:64, (h % 4) * 128:(h % 4 + 1) * 128]
                for p in range(2 if samp else 1):
                    CM = cmc[:, (1 + p) if samp else 0, :]
                    for h in range(8):
                        A("pool", lambda e, h=h, CM=CM: e.tensor_tensor(out=BCm[:], in0=BCb[:, hs(h)].unsqueeze(1).to_broadcast([128, 8, 64]), in1=CM.unsqueeze(2).to_broadcast([128, 8, 64]), op=ALU.mult), ("BCb", "cmc"), ("BCm",))
                        A("pe", lambda e, h=h: e.matmul(PS[6][0:64, :], lhsT=PUn[:, h, 0:64], rhs=BCm[:].rearrange("p c j -> p (c j)"), start=True, stop=True), ("PUn", "BCm"), ("ps6",))
                        A("act", lambda e, h=h: e.activation(out=GT[:, h, :, :], in_=PS[6][0:64, :].rearrange("p (c j) -> p c j", c=8), func=AF.Copy), ("ps6",), ("prj",))
                        A("pool", lambda e, h=h, CM=CM: e.tensor_tensor(out=Vm[:], in0=Vb[:, hs(h)].unsqueeze(1).to_broadcast([128, 8, 64]), in1=CM.unsqueeze(2).to_broadcast([128, 8, 64]), op=ALU.mult), ("Vb", "cmc"), ("pT",))
                        A("pool", lambda e, h=h, CM=CM: e.tensor_tensor(out=UVm[:], in0=PUn[:, h, 64:128].unsqueeze(1).to_broadcast([128, 8, 64]), in1=CM.unsqueeze(2).to_broadcast([128, 8, 64]), op=ALU.mult), ("PUn", "cmc"), ("pT",))
                        A("pe", lambda e, h=h: e.matmul(PS[7][0:64, :], lhsT=KCb[:, hs(h)], rhs=Vm[:].rearrange("p c j -> p (c j)"), start=True, stop=False), ("KCb", "pT"), ("ps7",))
                        A("pe", lambda e, h=h: e.matmul(PS[7][0:64, :], lhsT=BCb[:, hs(h)], rhs=UVm[:].rearrange("p c j -> p (c j)"), start=False, stop=True), ("BCb", "pT"), ("ps7",))
                        A("act", lambda e, h=h: e.activation(out=HS[:, h, :, :], in_=PS[7][0:64, :].rearrange("p (c j) -> p c j", c=8), func=AF.Copy), ("ps7",), ("prj",))
                    for c in range(8):
                        cidx = 8 * p + c
                        if samp:
                            Dm("sp", Sv[:], srw[l, cidx].rearrange("h i j -> i h j"), (), ("Sv",), "sld")
                            for h in range(8):
                                A("pe", lambda e, h=h: e.transpose(PS[4][0:64, hs(h)], Sv[:, h, :], ident[0:64, 0:64]), ("Sv", "ident"), ("ps4",))
                            A("dve", lambda e: e.tensor_copy(out=Mst[:], in_=PS[4][0:64, :].rearrange("p (h i) -> p h i", h=8)), ("ps4",), ("Mst",))
                            A("act", lambda e: e.activation(out=Mb[:], in_=Mst[:], func=AF.Copy), ("Mst",), ("Mb",))
                        for h in range(8):
                            A("pe", lambda e, h=h, c=c: e.matmul(PS[4][0:64, hs(h)], lhsT=GT[:, h, c, :], rhs=Mb[:, h, :], start=True, stop=False), ("prj", "Mb"), ("ps4",))
                            A("pe", lambda e, h=h, c=c: e.matmul(PS[4][0:64, hs(h)], lhsT=identb[0:64, 0:64], rhs=HS[:, h, c, :], start=False, stop=True), ("prj", "identb"), ("ps4",))
                        for h in range(8):
                            A("pe", lambda e, h=h, cidx=cidx: e.matmul(YTP(h)[:, cidx * CS:(cidx + 1) * CS], lhsT=Mb[:, h, :], rhs=RhT[:, h, cidx * CS:(cidx + 1) * CS], start=True, stop=True), ("Mb", "RhT"), ("ps%d" % (h // 4),))
                        A("dve", lambda e, cidx=cidx: e.tensor_tensor(out=Mst[:], in0=Mst[:], in1=GamT[:, :, cidx:cidx + 1].to_broadcast([64, 8, 64]), op=ALU.mult), ("Mst", "GamT"), ("Mst",))
                        A("dve", lambda e: e.tensor_tensor(out=Mst[:], in0=Mst[:], in1=PS[4][0:64, :].rearrange("p (h i) -> p h i", h=8), op=ALU.add), ("Mst", "ps4"), ("Mst",))
                        last_state = samp or (ti == NPT - 1 and c == 7)
                        if not samp:
                            A("act", lambda e: e.activation(out=Mb[:], in_=Mst[:], func=AF.Copy), ("Mst",), ("Mb",))
                        if last_state:
                            for h in range(8):
                                A("pe", lambda e, h=h: e.transpose(PS[4][0:64, hs(h)], Mst[:, h, :], ident[0:64, 0:64]), ("Mst", "ident"), ("ps4",))
                            A("act", lambda e: e.activation(out=Sv[:], in_=PS[4][0:64, :].rearrange("p (h j) -> p h j", h=8), func=AF.Copy), ("ps4",), ("Sv",))
                            dst = ss_o[l, cidx] if samp else sp_o[l]
                            Dm("sp", dst.rearrange("h i j -> i h j"), Sv[:], ("Sv",), (), "out")
                if KC <= 5:
                    return
                A("dve", lambda e: e.tensor_tensor(out=YL[:, 0:4, :], in0=YL[:, 0:4, :], in1=PS[0][0:64, :].rearrange("p (h t) -> p h t", h=4), op=ALU.add), ("osb", "ps0"), ("osb",))
                A("dve", lambda e: e.tensor_tensor(out=YL[:, 4:8, :], in0=YL[:, 4:8, :], in1=PS[1][0:64, :].rearrange("p (h t) -> p h t", h=4), op=ALU.add), ("osb", "ps1"), ("osb",))
                if STAGE <= 3:
                    return
                for h in range(8):
                    A("pe", lambda e, h=h: e.transpose(PS[5][:, hs(h)], YL[:, h, :], ident[0:64, 0:64]), ("osb", "ident"), ("ps5",))
                A("act", lambda e: e.activation(out=ya[:], in_=PS[5][:, :], func=AF.Copy), ("ps5",), ("ya",))
                if (not samp) and ti == 0:
                    A("pool", lambda e: e.memset(ya[0:96, :], 0.0), ("ya",), ("ya",))
                linear(prj, lambda k: xnT[:, k, :], 8, (wq_in[l], "wq_in%d" % l), RWC, NCOLS - RWC, ("xnT",), "prj", "p")
                Y3 = lambda tl: tl[:].rearrange("p (h j) -> p h j", h=8)
                A("dve", lambda e: e.tensor_reduce(out=s8[:], in_=Y3(ya), axis=AX.X, op=ALU.add), ("ya",), ("s8",))
                A("dve", lambda e: e.tensor_single_scalar(out=s8[:], in_=s8[:], scalar=-1.0 / 64, op=ALU.mult), ("s8",), ("s8",))
                A("dve", lambda e: e.tensor_tensor(out=Y3(ya), in0=Y3(ya), in1=s8[:].unsqueeze(2).to_broadcast([128, 8, 64]), op=ALU.add), ("ya", "s8"), ("ya",))
                A("dve", lambda e: e.tensor_tensor(out=t512[:], in0=ya[:], in1=ya[:], op=ALU.mult), ("ya",), ("t512",))
                A("dve", lambda e: e.tensor_reduce(out=s8b[:], in_=Y3(t512), axis=AX.X, op=ALU.add), ("t512",), ("s8b",))
                A("dve", lambda e: e.tensor_scalar(out=s8b[:], in0=s8b[:], scalar1=1.0 / 64, scalar2=64e-5, op0=ALU.mult, op1=ALU.add), ("s8b",), ("s8b",))
                rsqrt_ip(lambda: s8b[:], "s8b")
                A("dve", lambda e: e.tensor_tensor(out=Y3(ya), in0=Y3(ya), in1=s8b[:].unsqueeze(2).to_broadcast([128, 8, 64]), op=ALU.mult), ("ya", "s8b"), ("ya",))
                A("dve", lambda e: e.tensor_tensor(out=ya[:], in0=ya[:], in1=vv("lnx_g"), op=ALU.mult), ("ya", "vecB"), ("ya",))
                A("dve", lambda e: e.tensor_tensor(out=ya[:], in0=ya[:], in1=vv("lnx_b"), op=ALU.add), ("ya", "vecB"), ("ya",))
                A("dve", lambda e: e.tensor_tensor(out=t512[:], in0=hrw[:, C_R:C_R + 512], in1=kmt[:], op=ALU.mult), ("hrw", "kmt"), ("t512",))
                A("dve", lambda e: e.tensor_tensor(out=t512[:], in0=t512[:], in1=vv("r_k"), op=ALU.mult), ("t512", "vecB"), ("t512",))
                A("dve", lambda e: e.tensor_reduce(out=s8[:], in_=Y3(t512), axis=AX.X, op=ALU.add), ("t512",), ("s8",))
                A("dve", lambda e: e.tensor_tensor(out=Y3(t512), in0=hrw[:, C_V:C_V + 512].rearrange("p (h j) -> p h j", h=8), in1=s8[:].unsqueeze(2).to_broadcast([128, 8, 64]), op=ALU.mult), ("hrw", "s8"), ("t512",))
                A("dve", lambda e: e.tensor_tensor(out=ya[:], in0=ya[:], in1=t512[:], op=ALU.add), ("ya", "t512"), ("ya",))
                A("act", lambda e: e.activation(out=t512[:], in_=hrw[:, C_Z:C_Z + 512], func=AF.Silu), ("hrw",), ("t512",))
                A("dve", lambda e: e.tensor_tensor(out=ya[:], in0=ya[:], in1=t512[:], op=ALU.mult), ("ya", "t512"), ("ya",))

                if STAGE <= 4:
                    return
                for (c0, gname) in ((C_Q, "qg"), (C_KK, "kg")):
                    src = PQ(c0, 512)
                    A("dve", lambda e, src=src: e.tensor_tensor(out=t512[:], in0=src, in1=src, op=ALU.mult), ("prj",), ("t512",))
                    A("dve", lambda e: e.tensor_reduce(out=s8[:], in_=Y3(t512), axis=AX.X, op=ALU.add), ("t512",), ("s8",))
                    A("dve", lambda e: e.tensor_scalar(out=s8[:], in0=s8[:], scalar1=1.0 / 64, scalar2=1e-6, op0=ALU.mult, op1=ALU.add), ("s8",), ("s8",))
                    rsqrt_ip(lambda: s8[:], "s8")
                    A("dve", lambda e, src=src: e.tensor_tensor(out=src.rearrange("p (h j) -> p h j", h=8), in0=src.rearrange("p (h j) -> p h j", h=8), in1=s8[:].unsqueeze(2).to_broadcast([128, 8, 64]), op=ALU.mult), ("prj", "s8"), ("prj",))
                    gsl = vecB[:, VEC["misc"] * 512 + (0 if gname == "qg" else 64):VEC["misc"] * 512 + (64 if gname == "qg" else 128)]
                    A("dve", lambda e, src=src, gsl=gsl: e.tensor_tensor(out=src.rearrange("p (h j) -> p h j", h=8), in0=src.rearrange("p (h j) -> p h j", h=8), in1=gsl.unsqueeze(1).to_broadcast([128, 8, 64]), op=ALU.mult), ("prj", "vecB"), ("prj",))
                Dm("sp", k_o[l, ti * 128:(ti + 1) * 128, :], PQ(C_KK, 512), ("prj",), (), "out")
                Dm("sp", v_o[l, ti * 128:(ti + 1) * 128, :], PQ(C_VV, 512), ("prj",), (), "out")
                transpose_to(lambda j: qT[:, j, :], lambda j: PQ(C_Q + j * 128, 128), 4, ("prj",), "qT", dst4=lambda j0, nn: qT[:, j0:j0 + nn, :])
                if not samp:
                    transpose_to(lambda j: KT[:, j, ti * 128:(ti + 1) * 128], lambda j: PQ(C_KK + j * 128, 128), 4, ("prj",), "KT", dst4=lambda j0, nn: KT[:, j0:j0 + nn, ti * 128:(ti + 1) * 128])
                    A("pool", lambda e: e.tensor_copy(out=Vx[:, ti, :, 0:128], in_=PQ(C_VV, 512).rearrange("p (h j) -> p h j", h=4)), ("prj",), ("Vx",))
                    if ti == 0:
                        A("dve", lambda e: e.tensor_single_scalar(out=Vx[:, 0, :, 128], in_=io[:, 0:1].to_broadcast([128, 4]), scalar=112.0, op=ALU.is_ge), ("Vx", "io"), ("Vx",))
                else:
                    transpose_to(lambda j: ksT[:, j, :], lambda j: PQ(C_KK + j * 128, 128), 4, ("prj",), "ksT", dst4=lambda j0, nn: ksT[:, j0:j0 + nn, :])
                    A("pool", lambda e: e.tensor_copy(out=vsx[:, :, 0:128], in_=PQ(C_VV, 512).rearrange("p (h j) -> p h j", h=4)), ("prj",), ("vsx",))

                def attn_block(kT_fn, v_fn, bidx, nq, q0, first, last, rk, nkrows=128):
                    if KX <= 0:
                        return
                    for h in range(4):
                        for m in range(2):
                            A("pe", lambda e, h=h, m=m: e.matmul(PS[2 + m][0:nkrows, h * 128:h * 128 + nq], lhsT=kT_fn(h, m), rhs=qT[m * 64:(m + 1) * 64, h, q0:q0 + nq], start=True, stop=True), tuple(rk) + ("qT",), ("ps%d" % (2 + m),))
                    if KX <= 1:
                        return
                    for m in range(2):
                        A("dve", lambda e, m=m: e.scalar_tensor_tensor(out=sc[0:nkrows, m:8:2, 0:nq], in0=PS[2 + m][0:nkrows, :].rearrange("p (h q) -> p h q", h=4)[:, :, 0:nq], scalar=SCALE, in1=bias[0:nkrows, :, bidx, 0:nq], op0=ALU.mult, op1=ALU.add), ("ps%d" % (2 + m), "bias"), ("sc",))
                    if KX <= 2:
                        return
                    A("act", lambda e: e.activation(out=pT[0:nkrows, :, 0:nq], in_=sc[0:nkrows, :, 0:nq], func=AF.Exp), ("sc",), ("pT",))
                    if KX <= 3:
                        return
                    BK = [0, 1, 7]
                    for h in range(4):
                        for m in range(2):
                            hm = h * 2 + m
                            A("pe", lambda e, h=h, hm=hm: e.matmul(PS[BK[hm // 3]][0:nq, (hm % 3) * 160:(hm % 3) * 160 + 130], lhsT=pT[0:nkrows, hm, 0:nq], rhs=v_fn(h)[:, 0:130], start=True, stop=True), ("pT",) + tuple(rk), ("ps%d" % BK[hm // 3],))
                    for bi in range(3):
                        n3 = 3 if bi < 2 else 2
                        src = lambda bi=bi, n3=n3: PS[BK[bi]][0:nq, 0:n3 * 160].rearrange("p (a b) -> p a b", b=160)[:, :, 0:129]
                        dst = lambda bi=bi, n3=n3: osb[0:nq, bi * 3:bi * 3 + n3, 0:129]
                        if first:
                            A("dve", lambda e, src=src, dst=dst: e.tensor_copy(out=dst(), in_=src()), ("ps%d" % BK[bi], "osb"), ("osb",))
                        else:
                            A("dve", lambda e, src=src, dst=dst: e.tensor_tensor(out=dst(), in0=dst(), in1=src(), op=ALU.add), ("ps%d" % BK[bi], "osb"), ("osb",))

                if KSUB == 0 or (KSUB == 1 and samp):
                    return
                if not samp:
                    nb = ti + 1
                    for j in range(nb):
                        dlt = ti - j
                        bidx = 0 if dlt == 0 else (1 if dlt == 1 else 2)
                        attn_block(lambda h, m, j=j: KT[m * 64:(m + 1) * 64, h, j * 128:(j + 1) * 128], lambda h, j=j: Vx[:, j, h, :], bidx, 128, 0, j == 0, j == nb - 1, ("KT", "Vx"))
                    A("dve", lambda e: e.tensor_single_scalar(out=rl[:], in_=osb[:, :, 128], scalar=1e-30, op=ALU.max), ("osb",), ("rl",))
                    A("dve", lambda e: e.reciprocal(out=rl[:], in_=rl[:]), ("rl",), ("rl",))
                else:
                    A("pool", lambda e: e.memset(vpxB[:, :, 128:130], 1.0), ("PUn",), ("PUn",))
                    SETS = [dict(kp=wt[:], kk=("wt",), vp=at[:], vk=("at",), kT=kpT, kTk="kpT", vx=vpx, vxk="vpx", sk="A"),
                            dict(kp=kpgB, kk=("Vb", "Rpb"), vp=vpgB, vk=("Pmb", "BCb"), kT=kpTB, kTk="KCb", vx=vpxB, vxk="PUn", sk="B")]
                    pgc = 0
                    for sq in range(16):
                        for pg in range(NPG):
                            col = sq * NPG + pg
                            S_ = SETS[pgc % 2]
                            pgc += 1
                            kb.add("pool", lambda e, col=col, S_=S_: e.indirect_dma_start(out=S_["kp"], out_offset=None, in_=ck, in_offset=bass.IndirectOffsetOnAxis(ap=(pidx1 if l else pidx)[:, col:col + 1], axis=0)), reads=("pidx",), writes=S_["kk"], dma=True, semkey="gk" + S_["sk"])
                            kb.add("pool", lambda e, col=col, S_=S_: e.indirect_dma_start(out=S_["vp"], out_offset=None, in_=cv, in_offset=bass.IndirectOffsetOnAxis(ap=(pidx1 if l else pidx)[:, col:col + 1], axis=0)), reads=("pidx",), writes=S_["vk"], dma=True, semkey="gv" + S_["sk"])
                            transpose_to(None, lambda j, S_=S_: S_["kp"][:, j * 128:(j + 1) * 128], 4, S_["kk"], S_["kTk"], ps_i=6, dst4=lambda j0, nn, S_=S_: S_["kT"][:, j0:j0 + nn, :])
                            A("act", lambda e, S_=S_: e.activation(out=S_["vx"][:, :, 0:128], in_=S_["vp"].rearrange("p (h j) -> p h j", h=4), func=AF.Copy), S_["vk"], (S_["vxk"],))
                            bidx = 3 if pg == NPG - 1 else 2
                            attn_block(lambda h, m, S_=S_: S_["kT"][m * 64:(m + 1) * 64, h, :], lambda h, S_=S_: S_["vx"][:, h, :], bidx, 8, sq * 8, pg == 0, False, (S_["kTk"], S_["vxk"]))
                        Dm("sp", vs8[:], vsx[sq * 8:sq * 8 + 8, :, :], ("vsx",), ("vs8",), "vs8")
                        attn_block(lambda h, m, sq=sq: ksT[m * 64:(m + 1) * 64, h, sq * 8:sq * 8 + 8], lambda h: vs8[:, h, :], 4, 8, sq * 8, False, True, ("ksT", "vs8"), nkrows=8)
                        A("dve", lambda e: e.reciprocal(out=rl[0:8, :], in_=osb[0:8, :, 128]), ("osb",), ("osb8",))
                        A("dve", lambda e: e.tensor_tensor(out=osb[0:8, :, 0:128], in0=osb[0:8, :, 0:128], in1=rl[0:8, :].unsqueeze(2).to_broadcast([8, 8, 128]), op=ALU.mult), ("osb8", "osb"), ("osb8", "osb"))
                        Dm("sp", xs[0:1, 0:1], xs[0:1, 0:1], (), (), "nopd") if False else None
                        A("dve", lambda e: e.scalar_tensor_tensor(out=osb[0:8, 0:8:2, 0:128], in0=osb[0:8, 1:8:2, 0:128], scalar=lamn[0:8, 0:1], in1=osb[0:8, 0:8:2, 0:128], op0=ALU.mult, op1=ALU.add), ("osb8", "osb", "lamn"), ("osb8", "osb"))
                        Dm("sp", o2[sq * 8:sq * 8 + 8, :].rearrange("p (h e) -> p h e", h=4), osb[0:8, 0:8:2, 0:128], ("osb8", "osb", "o2s"), ("o2s",), "o2d")
                if not samp:
                    A("dve", lambda e: e.tensor_tensor(out=osb[:, :, 0:128], in0=osb[:, :, 0:128], in1=rl[:].unsqueeze(2).to_broadcast([128, 8, 128]), op=ALU.mult), ("osb", "rl"), ("osb",))
                    A("dve", lambda e: e.scalar_tensor_tensor(out=o2[:].rearrange("p (h e) -> p h e", h=4), in0=osb[:, 1:8:2, 0:128], scalar=lamn[:, 0:1], in1=osb[:, 0:8:2, 0:128], op0=ALU.mult, op1=ALU.add), ("osb", "lamn"), ("o2",))
                o2k = ("o2s",) if samp else ("o2",)
                O4 = lambda tl: tl[:].rearrange("p (h e) -> p h e", h=4)
                A("dve", lambda e: e.tensor_tensor(out=t512[:], in0=o2[:], in1=o2[:], op=ALU.mult), o2k, ("t512",))
                A("dve", lambda e: e.tensor_reduce(out=s8[:, 0:4], in_=O4(t512), axis=AX.X, op=ALU.add), ("t512",), ("s8",))
                A("dve", lambda e: e.tensor_scalar(out=s8[:, 0:4], in0=s8[:, 0:4], scalar1=1.0 / 128, scalar2=1e-6, op0=ALU.mult, op1=ALU.add), ("s8",), ("s8",))
                rsqrt_ip(lambda: s8[:, 0:4], "s8")
                A("dve", lambda e: e.tensor_tensor(out=O4(yb), in0=O4(o2), in1=s8[:, 0:4].unsqueeze(2).to_broadcast([128, 4, 128]), op=ALU.mult), o2k + ("s8",), ("yb",))
                sg_ap = vecB[:, VEC["misc"] * 512 + 128:VEC["misc"] * 512 + 256]
                A("dve", lambda e: e.tensor_tensor(out=O4(yb), in0=O4(yb), in1=sg_ap.unsqueeze(1).to_broadcast([128, 4, 128]), op=ALU.mult), ("yb", "vecB"), ("yb",))
                A("act", lambda e: e.activation(out=t512[:], in_=PQ(C_ZB, 512), func=AF.Silu), ("prj",), ("t512",))
                A("dve", lambda e, li=li: e.scalar_tensor_tensor(out=yb[:], in0=yb[:], scalar=1.0 - li, in1=t512[:], op0=ALU.mult, op1=ALU.mult), ("yb", "t512"), ("yb",))
                if STAGE <= 5:
                    return
                transpose_to(lambda j: yT[:, j, :], lambda j: ya[:, j * 128:(j + 1) * 128], 4, ("ya",), "xsT", dst4=lambda j0, nn: yT[:, j0:j0 + nn, :])
                transpose_to(lambda j: yT[:, 4 + j, :], lambda j: yb[:, j * 128:(j + 1) * 128], 4, ("yb",), "xsT", dst4=lambda j0, nn: yT[:, 4 + j0:4 + j0 + nn, :])
                linear(mm, lambda k: yT[:, k, :], 4, (wq_a[l], "wq_a%d" % l), 0, D, ("xsT",), "mm", "p")
                A("act", lambda e: e.activation(out=junk[:], in_=PQ(C_GA, D), func=AF.Sigmoid), ("prj",), ("junk",))
                A("dve", lambda e: e.tensor_tensor(out=junk[:], in0=junk[:], in1=mm[:], op=ALU.mult), ("junk", "mm"), ("junk",))
                linear(mm, lambda k: yT[:, 4 + k, :], 4, (wq_b[l], "wq_b%d" % l), 0, D, ("xsT",), "mm", "p")
                A("act", lambda e: e.activation(out=xn[:], in_=PQ(C_GB, D), func=AF.Sigmoid), ("prj", "xnT"), ("xn",))
                A("dve", lambda e: e.tensor_tensor(out=mm[:], in0=mm[:], in1=xn[:], op=ALU.mult), ("xn", "mm"), ("mm",))
                A("dve", lambda e: e.tensor_tensor(out=mm[:], in0=mm[:], in1=junk[:], op=ALU.add), ("junk", "mm"), ("mm",))
                transpose_to(lambda j: mT[:, j, :], lambda j: mm[:, j * 128:(j + 1) * 128], 8, ("mm",), "xsT", dst4=lambda j0, nn: mT[:, j0:j0 + nn, :])
                linear(junk, lambda k: mT[:, k, :], 8, (wq_o[l], "wq_o%d" % l), 0, D, ("xsT",), "junk", "p")
                A("dve", lambda e: e.tensor_tensor(out=xt[:], in0=xt[:], in1=junk[:], op=ALU.add), ("xt", "junk"), ("xt",))
                if l == 0:
                    Dm("sp", xs[ti * 128:(ti + 1) * 128, :], xt[:], ("xt",), (xk,), "xso")
                else:
                    Dm("sp", y_o[ti * 128:(ti + 1) * 128, :], xt[:], ("xt",), (), "out")
            for ti in range(NT):
                tile_body(l, ti, li)
        kb.emit_all()
    return nc


_CACHE = {}


def _pack_vec(ins, l):
    v = np.zeros((16, 512), np.float32)
    f = lambda a: np.asarray(a, np.float32).reshape(-1)
    v[0] = f(ins["norm_g"][l])[:512]; v[1] = f(ins["norm_g"][l])[512:]
    mu = np.zeros(2560, np.float32); mu[:RWC] = f(ins["mu_shift"][l])
    v[2:7] = mu.reshape(5, 512)
    v[7] = f(ins["w0"][l]); v[8] = f(ins["a0"][l]); v[9] = f(ins["k_k"][l]); v[10] = f(ins["k_a"][l])
    v[11] = f(ins["r_k"][l]); v[12] = f(ins["lnx_g"][l]); v[13] = f(ins["lnx_b"][l])
    v[14, 0:64] = f(ins["q_norm_g"][l]); v[14, 64:128] = f(ins["k_norm_g"][l]); v[14, 128:256] = f(ins["subln_g"][l])
    return v.reshape(1, -1)


def kernel(**ins):
    ins = {k: np.asarray(v) for k, v in ins.items()}
    B, SEQ, _ = ins["x_prompt"].shape
    DB, DS, _ = ins["x_sample"].shape
    NPG = ins["page_table"].shape[1]
    NPOOL = ins["cache_k"].shape[1]
    NPT = SEQ // 128 + 1
    NT = NPT + 1
    ncores = B
    assert DB == 16 * ncores and DS == 8
    key = (NPT, NPG, NPOOL)
    if key not in _CACHE:
        _CACHE[key] = build(*key)
    nc = _CACHE[key]
    mats, msk = make_consts()
    cmk_np, cmc_np = make_chunk_consts()
    bm_in = np.ascontiguousarray(mats.transpose(1, 0, 2)); mk_in = np.ascontiguousarray(msk.transpose(1, 0, 2))
    ck = ins["cache_k"].reshape(2 * NPOOL * 128, 512); cv = ins["cache_v"].reshape(2 * NPOOL * 128, 512)
    vec = np.stack([_pack_vec(ins, l) for l in range(2)])
    lamv = np.stack([np.concatenate([ins["lam_q1"][l], ins["lam_q2"][l], ins["lam_k1"][l], ins["lam_k2"][l]]).reshape(1, 256) for l in range(2)]).astype(np.float32)
    in_maps = []
    for c in range(ncores):
        xin = np.zeros((NT * 128, D), np.float32)
        xin[112:128] = ins["meta_tokens"]
        xin[128:128 + SEQ] = ins["x_prompt"][c]
        xin[NPT * 128:] = ins["x_sample"][16 * c:16 * c + 16].reshape(128, D)
        in_maps.append(dict(
            xin=xin, ck=ck, cv=cv, pt=ins["page_table"][16 * c:16 * c + 16].reshape(1, -1).astype(np.int32),
            srw=np.ascontiguousarray(ins["state_rwkv"][:, 16 * c:16 * c + 16]), ssh=np.ascontiguousarray(ins["state_shift"][:, 16 * c:16 * c + 16]),
            relb=ins["rel_bias"].reshape(1, 128).astype(np.float32),
            w_in=ins["w_in"], w_a=ins["w_a_out"], w_b=ins["w_b_out"], w_o=ins["w_o"], w2=ins["w2"], a2=ins["a2"],
            vec=vec, lamv=lamv, bm_in=bm_in, mk_in=mk_in, id_in=np.eye(128, dtype=np.float32), cmk_in=cmk_np, cmc_in=cmc_np))
    _r = run_bass_kernel_spmd(nc, in_maps, core_ids=list(range(ncores)), **({'trace': True} if os.environ.get('KTRACE') else {}))
    if os.environ.get('KTRACE'):
        print('EXEC_TIME_NS', _r.exec_time_ns)
    res = _r.results
    T = SEQ + 16
    yp = np.stack([r["y_o"][128:128 + SEQ] for r in res])
    ys = np.concatenate([r["y_o"][NPT * 128:].reshape(16, 8, D) for r in res])
    kp = np.stack([r["k_o"][:, 112:112 + T].reshape(2, T, 4, 128) for r in res], axis=1)
    vp = np.stack([r["v_o"][:, 112:112 + T].reshape(2, T, 4, 128) for r in res], axis=1)
    sp = np.stack([r["sp_o"] for r in res], axis=1)
    hp = np.stack([r["hp_o"].reshape(2, D) for r in res], axis=1)
    ks = np.concatenate([r["k_o"][:, NPT * 128:].reshape(2, 16, 8, 4, 128) for r in res], axis=1)
    vs = np.concatenate([r["v_o"][:, NPT * 128:].reshape(2, 16, 8, 4, 128) for r in res], axis=1)
    ss = np.concatenate([r["ss_o"] for r in res], axis=1)
    hs = np.concatenate([r["hs_o"] for r in res], axis=1)
    return (yp, ys, kp, vp, sp, hp, ks, vs, ss, hs)
```

```python
import math
import contextlib
import numpy as np
import concourse.bass as bass
import concourse.mybir as mybir
from concourse.bass_utils import run_bass_kernel_spmd

F32 = mybir.dt.float32
BF16 = mybir.dt.bfloat16
I32 = mybir.dt.int32
AF = mybir.ActivationFunctionType
ALU = mybir.AluOpType
AX = mybir.AxisListType
ENGS = ("pe", "act", "dve", "pool", "sp")


class Op:
    __slots__ = ("eng", "emit", "deps", "inc", "semkey", "count", "is_dma", "waits")

    def __init__(self, eng, emit, is_dma, semkey):
        self.eng = eng; self.emit = emit; self.deps = []; self.inc = False
        self.semkey = semkey; self.count = 0; self.is_dma = is_dma; self.waits = []


class KB:
    def __init__(self, nc):
        self.nc = nc
        self.ops = {e: [] for e in ENGS}
        self.last_w = {}; self.readers = {}; self.dma_cnt = {}

    def add(self, eng, emit, reads=(), writes=(), dma=False, semkey=None):
        if dma and semkey is None:
            semkey = "dma_" + eng
        op = Op(eng, emit, dma, semkey)
        deps = []
        for k in reads:
            w = self.last_w.get(k)
            if w is not None:
                deps.append(w)
        for k in writes:
            w = self.last_w.get(k)
            if w is not None:
                deps.append(w)
            deps.extend(self.readers.get(k, ()))
        seen = set()
        for d in deps:
            if id(d) in seen:
                continue
            seen.add(id(d))
            if d.eng == "pe" and eng == "pe" and not d.is_dma and not dma:
                continue
            if d.is_dma:
                op.waits.append((d.semkey, self.dma_cnt[d.semkey] * 16))
            else:
                d.inc = True
                op.deps.append(d)
        for k in reads:
            self.readers.setdefault(k, []).append(op)
        for k in writes:
            self.last_w[k] = op
            self.readers[k] = []
        if dma:
            self.dma_cnt[semkey] = self.dma_cnt.get(semkey, 0) + 1
        self.ops[eng].append(op)
        return op

    def emit_all(self):
        nc = self.nc
        for e in ENGS:
            c = 0
            for op in self.ops[e]:
                if not op.is_dma and op.inc:
                    c += 1
                    op.count = c
        semkeys = sorted(self.dma_cnt.keys())
        with contextlib.ExitStack() as st:
            esem = {e: st.enter_context(nc.semaphore("s_" + e)) for e in ENGS}
            dsem = {k: st.enter_context(nc.semaphore("d%d" % i)) for i, k in enumerate(semkeys)}
            block = st.enter_context(nc.Block())
            reg = {"pe": block.tensor, "act": block.scalar, "dve": block.vector,
                   "pool": block.gpsimd, "sp": block.sync}
            for e in ENGS:
                def body(eng, ops=self.ops[e], e=e):
                    seen_e = {x: 0 for x in ENGS}
                    seen_d = {}
                    for op in ops:
                        for (k, v) in op.waits:
                            if v > seen_d.get(k, 0):
                                seen_d[k] = v
                                eng.wait_ge(dsem[k], v)
                        need = {}
                        for d in op.deps:
                            if d.count > seen_e[d.eng]:
                                need[d.eng] = max(need.get(d.eng, 0), d.count)
                        for de, v in need.items():
                            seen_e[de] = v
                            eng.wait_ge(esem[de], v)
                        ins = op.emit(eng)
                        if op.is_dma:
                            ins.then_inc(dsem[op.semkey], 16)
                        elif op.inc:
                            ins.then_inc(esem[e], 1)
                    if e == "sp":
                        for k in semkeys:
                            eng.wait_ge(dsem[k], self.dma_cnt[k] * 16)
                reg[e](body)


D = 1024
NCOLS = 6272
RWC = 2176
C_R, C_K, C_V, C_WD, C_Z = 0, 512, 1024, 1536, 1664
C_Q, C_KK, C_VV, C_ZB, C_GA, C_GB = 2176, 2688, 3200, 3712, 4224, 5248
NEGM = -30000.0
import os
STAGE = int(os.environ.get('KSTAGE', '99'))
KSUB = int(os.environ.get('KSUB', '9'))
KX = int(os.environ.get('KX', '9'))
KC = int(os.environ.get('KC', '9'))
KH = int(os.environ.get('KH', '9'))


def t5_bucket_np(d):
    d = np.maximum(d, 0)
    nf = np.maximum(d, 1).astype(np.float32)
    large = 16 + (np.log(nf / 16) / np.float32(math.log(128 / 16)) * 16).astype(np.int32)
    large = np.minimum(large, 31)
    return np.where(d < 16, d, large)


def make_consts():
    k = np.arange(128)[:, None]
    q = np.arange(128)[None, :]
    mats = np.zeros((5, 128, 128), np.float32)
    msk = np.zeros((5, 128, 128), np.float32)
    d0 = q - k
    mats[0] = np.where(d0 >= 0, t5_bucket_np(d0), -1)
    msk[0] = np.where(d0 >= 0, 0.0, NEGM)
    mats[1] = t5_bucket_np(128 + q - k)
    mats[2] = 31
    tq = q % 8
    mats[3] = t5_bucket_np(128 + tq - k)
    d4 = tq - (k % 8)
    mats[4] = np.where(d4 >= 0, t5_bucket_np(d4), -1)
    msk[4] = np.where(d4 >= 0, 0.0, NEGM)
    return mats, msk


def make_chunk_consts():
    s_ = np.arange(128)[:, None]; t_ = np.arange(128)[None, :]
    cmk = np.zeros((8, 128, 128), np.float32)
    for i, cs in enumerate((16, 8)):
        same = (s_ // cs) == (t_ // cs)
        cmk[0 + i] = same & (s_ <= t_)
        cmk[2 + i] = same & (s_ < t_)
        cmk[4 + i] = same & (t_ < s_)
        cmk[6 + i] = same
    cmc = np.zeros((3, 128, 8), np.float32)
    r = np.arange(128)
    for c in range(8):
        cmc[0, :, c] = (r // 16) == c
        cmc[1, :, c] = (r // 8) == c
        cmc[2, :, c] = (r // 8) == 8 + c
    return np.ascontiguousarray(cmk.transpose(1, 0, 2)), np.ascontiguousarray(cmc.transpose(1, 0, 2))


def build(NPT, NPG, NPOOL):
    NT = NPT + 1
    nc = bass.Bass("TRN2", target_bir_lowering=False)
    di = lambda n, s, d=F32: nc.dram_tensor(n, s, d, kind="ExternalInput").ap()
    do = lambda n, s, d=F32: nc.dram_tensor(n, s, d, kind="ExternalOutput").ap()
    xin = di("xin", [NT * 128, D])
    ck = di("ck", [2 * NPOOL * 128, 512]); cv = di("cv", [2 * NPOOL * 128, 512])
    pt = di("pt", [1, 16 * NPG], I32)
    srw = di("srw", [2, 16, 8, 64, 64]); ssh = di("ssh", [2, 16, D])
    relb = di("relb", [1, 128])
    w_in = di("w_in", [2, D, NCOLS]); w_a = di("w_a", [2, 512, D]); w_b = di("w_b", [2, 512, D]); w_o = di("w_o", [2, D, D])
    w2 = di("w2", [2, 64, 512]); a2 = di("a2", [2, 64, 512])
    vec = di("vec", [2, 1, 16 * 512])
    lamv = di("lamv", [2, 1, 256])
    bm_in = di("bm_in", [128, 5, 128]); mk_in = di("mk_in", [128, 5, 128]); id_in = di("id_in", [128, 128])
    cmk_in = di("cmk_in", [128, 8, 128]); cmc_in = di("cmc_in", [128, 3, 8])
    y_o = do("y_o", [NT * 128, D])
    k_o = do("k_o", [2, NT * 128, 512]); v_o = do("v_o", [2, NT * 128, 512])
    sp_o = do("sp_o", [2, 8, 64, 64]); hp_o = do("hp_o", [2, 1, D])
    ss_o = do("ss_o", [2, 16, 8, 64, 64]); hs_o = do("hs_o", [2, 16, D])
    xs = nc.dram_tensor("xs", [NT * 128, D], F32, kind="Internal").ap()
    wq_in = nc.dram_tensor("wq_in", [2, D, NCOLS], BF16, kind="Internal").ap(); wq_a = nc.dram_tensor("wq_a", [2, 512, D], BF16, kind="Internal").ap()
    wq_b = nc.dram_tensor("wq_b", [2, 512, D], BF16, kind="Internal").ap(); wq_o = nc.dram_tensor("wq_o", [2, D, D], BF16, kind="Internal").ap()

    kb = KB(nc)
    A = lambda eng, fn, r=(), w=(): kb.add(eng, fn, reads=r, writes=w)
    Dm = lambda eng, out, in_, r, w, key: kb.add(eng, lambda e: e.dma_start(out=out, in_=in_), reads=r, writes=w, dma=True, semkey=key)

    with contextlib.ExitStack() as st:
        sb = lambda n, s, d=F32: st.enter_context(nc.sbuf_tensor(n, s, d))
        pst = lambda n: st.enter_context(nc.psum_tensor(n, [128, 512], F32))
        PS = [pst("ps%d" % i) for i in range(8)]
        ident = sb("ident", [128, 128]); identb = sb("identb", [128, 128], BF16)
        rbB = sb("rbB", [128, 128])
        bias = sb("bias", [128, 4, 5, 128], BF16)
        vecB = sb("vecB", [128, 15 * 512])
        lamB = sb("lamB", [128, 256]); lamT = sb("lamT", [128, 4]); lam = sb("lam", [128, 1]); lamn = sb("lamn", [128, 1])
        xt = sb("xt", [128, D]); xn = sb("xn", [128, D]); xsft = sb("xsft", [128, D]); junk = sb("junk", [128, D])
        ssq = sb("ssq", [128, 1]); rstd = sb("rstd", [128, 1])
        xnT = sb("xnT", [128, 8, 128], BF16); xsT = sb("xsT", [128, 8, 128], BF16)
        wbfs = [sb("wbf%d" % i, [128, 8, 256], BF16) for i in range(3)]
        hrw = sb("hrw", [128, RWC]); prj = sb("prj", [128, NCOLS - RWC])
        GT = prj[0:64, 0:2048].bitcast(BF16).rearrange("p (h c j) -> p h c j", h=8, c=8); HS = prj[0:64, 2048:4096].bitcast(BF16).rearrange("p (h c j) -> p h c j", h=8, c=8)
        bm = prj[:, 0:640].rearrange("p (a b) -> p a b", a=5); mk = prj[:, 640:1280].rearrange("p (a b) -> p a b", a=5); eqt = prj[:, 1280:1920].rearrange("p (a b) -> p a b", a=5)
        w2a2b = sb("w2a2b", [128, 512], BF16)
        wdT = sb("wdT", [128, 128], BF16)
        wt = sb("wt", [128, 512]); at = sb("at", [128, 512]); kkt = sb("kkt", [128, 512]); bt = sb("bt", [128, 512]); kmt = sb("kmt", [128, 512])
        t512 = sb("t512", [128, 512]); s8 = sb("s8", [128, 8]); s8b = sb("s8b", [128, 8])
        cmk = sb("cmk", [128, 8, 128]); cmc = sb("cmc", [128, 3, 8])
        TB = sb("TB", [128, 5, 512], BF16)
        Vb = TB[:, 0, :]; Rpb = TB[:, 1, :]; Pmb = TB[:, 2, :]; BCb = TB[:, 3, :]; KCb = TB[:, 4, :]
        kpgB = TB[:, 0:2, :].rearrange("p a c -> p (a c)").bitcast(F32); vpgB = TB[:, 2:4, :].rearrange("p a c -> p (a c)").bitcast(F32)
        kpTB = TB[:, 4, :].rearrange("p (h t) -> p h t", h=4)
        RpT = sb("RpT", [64, 8, 128], BF16); PmT = sb("PmT", [64, 8, 128], BF16); BnT = sb("BnT", [64, 8, 128], BF16); KnT = sb("KnT", [64, 8, 128], BF16)
        GamT = sb("GamT", [64, 8, 16]); MA_ = [sb("MA%d" % i, [128, 4, 128], BF16) for i in range(2)]; DB_ = [sb("DB%d" % i, [128, 9, 128], BF16) for i in range(2)]; RR_ = [sb("RR%d" % i, [128, 4, 128], BF16) for i in range(2)]
        PUn = sb("PUn", [128, 8, 128], BF16);
        vpxB = PUn[:].rearrange("p (h a) t -> p h (a t)", a=2)[:, :, 0:132]
        RhT = sb("RhT", [64, 8, 128], BF16); Mst = sb("Mst", [64, 8, 64]); Mb = sb("Mb", [64, 8, 64], BF16); Sv = sb("Sv", [64, 8, 64])
        ya = sb("ya", [128, 512]); yb = sb("yb", [128, 512])
        KT = sb("KT", [128, 4, NPT * 128], BF16); Vx = sb("Vx", [128, NPT, 4, 132], BF16)
        qT = sb("qT", [128, 4, 128], BF16)
        kpT = sb("kpT", [128, 4, 128], BF16); vpx = sb("vpx", [128, 4, 132], BF16)
        ksT = sb("ksT", [128, 4, 128], BF16); vsx = sb("vsx", [128, 4, 132], BF16); vs8 = sb("vs8", [8, 4, 132], BF16)
        sc = sb("sc", [128, 8, 128]); pT = sb("pT", [128, 8, 128], BF16)
        _pv = lambda a: pT[:, a:a + 4, :].rearrange("p a (b j) -> p (a b) j", b=2)
        Vm = _pv(0); UVm = _pv(4); BCm = sb("BCm", [128, 8, 64], BF16)
        cwt = sc[:, 0:4, :].rearrange("p a t -> p (a t)")
        osb = sb("osb", [128, 8, 132]); rl = sb("rl", [128, 8]); o2 = sb("o2", [128, 512])
        YL = osb[0:64, :, 0:128]
        yT = xsT
        mm = sb("mm", [128, D]); mT = xsT
        ptb = sb("ptb", [128, 16 * NPG], I32); io = sb("io", [128, 1], I32); pidx = sb("pidx", [128, 16 * NPG], I32); pidx1 = sb("pidx1", [128, 16 * NPG], I32)

        Dm("sp", ident[:], id_in, (), ("ident",), "c0")
        Dm("sp", bm[:], bm_in, (), ("bm", "prj"), "c0")
        Dm("sp", mk[:], mk_in, (), ("mk", "prj"), "c0")
        Dm("sp", rbB[:], relb.partition_broadcast(128), (), ("rbB",), "c0")
        Dm("sp", cmk[:], cmk_in, (), ("cmk",), "c0")
        Dm("sp", cmc[:], cmc_in, (), ("cmc",), "c0")
        Dm("sp", ptb[:], pt.partition_broadcast(128), (), ("ptb",), "c0")
        A("dve", lambda e: e.tensor_copy(out=identb[:], in_=ident[:]), ("ident",), ("identb",))
        A("pool", lambda e: e.iota(io[:], pattern=[[0, 1]], base=0, channel_multiplier=1), (), ("io",))
        A("dve", lambda e: e.tensor_scalar(out=pidx[:], in0=ptb[:], scalar1=128.0, scalar2=io[:, 0:1], op0=ALU.mult, op1=ALU.add), ("ptb", "io"), ("pidx",))
        A("dve", lambda e: e.tensor_single_scalar(out=pidx1[:], in_=pidx[:], scalar=float(NPOOL * 128), op=ALU.add), ("pidx",), ("pidx",))
        for h in range(4):
            A("dve", lambda e, h=h: e.tensor_copy(out=bias[:, h, :, :], in_=mk[:]), ("mk", "prj"), ("bias",))
        for b in range(32):
            A("dve", lambda e, b=b: e.tensor_single_scalar(out=eqt[:], in_=bm[:], scalar=float(b), op=ALU.is_equal), ("bm", "prj"), ("eqt", "prj"))
            for h in range(4):
                A("dve", lambda e, b=b, h=h: e.scalar_tensor_tensor(out=bias[:, h, :, :], in0=eqt[:], scalar=rbB[:, b * 4 + h:b * 4 + h + 1], in1=bias[:, h, :, :], op0=ALU.mult, op1=ALU.add), ("eqt", "prj", "rbB", "bias"), ("bias",))
        for ti in range(NT):
            Dm("sp", xt[:], xin[ti * 128:(ti + 1) * 128, :], (), ("xt",), "xin")
            Dm("sp", xs[ti * 128:(ti + 1) * 128, :], xt[:], ("xt",), ("xs%d" % ti,), "xso")
        A("pool", lambda e: e.memset(Vx[:], 0.0), (), ("Vx",))
        A("pool", lambda e: e.memset(Vx[:, :, :, 128:129], 1.0), ("Vx",), ("Vx",))
        A("pool", lambda e: e.memset(vpx[:], 1.0), (), ("vpx",))
        A("pool", lambda e: e.memset(vsx[:], 1.0), (), ("vsx",))

        tr_ctr = [0]

        def transpose_to(dst_fn, src_ap_fn, n, rkeys, wkey, ps_i=5, dst4=None):
            for j0 in range(0, n, 4):
                nn = min(4, n - j0)
                pb = ps_i
                if ps_i == 5:
                    pb = 5 + (tr_ctr[0] % 2)
                    tr_ctr[0] += 1
                for j in range(nn):
                    A("pe", lambda e, j=j, j0=j0, pb=pb: e.transpose(PS[pb][:, j * 128:(j + 1) * 128], src_ap_fn(j0 + j), ident[:]), tuple(rkeys) + ("ident",), ("ps%d" % pb,))
                if dst4 is not None:
                    A("act", lambda e, j0=j0, nn=nn, pb=pb: e.activation(out=dst4(j0, nn), in_=PS[pb][:, 0:nn * 128].rearrange("p (a t) -> p a t", a=nn), func=AF.Copy), ("ps%d" % pb,), (wkey,))
                else:
                    for j in range(nn):
                        A("act", lambda e, j=j, j0=j0, pb=pb: e.activation(out=dst_fn(j0 + j), in_=PS[pb][:, j * 128:(j + 1) * 128], func=AF.Copy), ("ps%d" % pb,), (wkey,))

        slot_ctr = [0]

        def linear(dst, lhsT_fn, nk, wsrc, c0, ncols, rkeys, wkey, pname, evac_eng="act"):
            wq, wqk = wsrc
            for g0 in range(0, ncols, 256):
                gw = min(256, ncols - g0)
                sl = slot_ctr[0] % 3
                pb = 6 + (slot_ctr[0] % 2)
                slot_ctr[0] += 1
                wt_ = wbfs[sl]
                Dm("sp", wt_[:, 0:nk, 0:gw], wq[:, c0 + g0:c0 + g0 + gw].rearrange("(k p) c -> p k c", p=128), (wqk,), ("wbf%d" % sl,), "wl%d" % sl)
                for k in range(nk):
                    A("pe", lambda e, k=k, gw=gw, wt_=wt_, pb=pb: e.matmul(PS[pb][:, 0:gw], lhsT=lhsT_fn(k), rhs=wt_[:, k, 0:gw], start=(k == 0), stop=(k == nk - 1)), tuple(rkeys) + ("wbf%d" % sl,), ("ps%d" % pb,))
                A(evac_eng, (lambda e, g0=g0, gw=gw, pb=pb: e.activation(out=dst[:, g0:g0 + gw], in_=PS[pb][:, 0:gw], func=AF.Copy)), ("ps%d" % pb,), (wkey,))

        stg = hrw[:, 0:2048].rearrange("p (k c) -> p k c", k=8)
        cvt = 0
        for l in range(2):
            for (src, dstq, nk, ncols, qk) in ((w_in[l], wq_in[l], 8, NCOLS, "wq_in%d" % l), (w_a[l], wq_a[l], 4, D, "wq_a%d" % l), (w_b[l], wq_b[l], 4, D, "wq_b%d" % l), (w_o[l], wq_o[l], 8, D, "wq_o%d" % l)):
                for g0 in range(0, ncols, 256):
                    gw = min(256, ncols - g0)
                    sl = cvt % 3
                    ce = ("act", "dve", "pool")[cvt % 3]
                    cvt += 1
                    Dm("sp", stg[:, 0:nk, 0:gw], src[:, g0:g0 + gw].rearrange("(k p) c -> p k c", p=128), (), ("hrw",), "wst")
                    if ce == "act":
                        A("act", lambda e, sl=sl, nk=nk, gw=gw: e.activation(out=wbfs[sl][:, 0:nk, 0:gw], in_=stg[:, 0:nk, 0:gw], func=AF.Copy), ("hrw",), ("wbf%d" % sl,))
                    else:
                        A(ce, lambda e, sl=sl, nk=nk, gw=gw: e.tensor_copy(out=wbfs[sl][:, 0:nk, 0:gw], in_=stg[:, 0:nk, 0:gw]), ("hrw",), ("wbf%d" % sl,))
                    Dm("sp", dstq[:, g0:g0 + gw].rearrange("(k p) c -> p k c", p=128), wbfs[sl][:, 0:nk, 0:gw], ("wbf%d" % sl,), (qk,), "wqo")

        VEC = {n: i for i, n in enumerate(["norm_g", "norm_g2", "mu0", "mu1", "mu2", "mu3", "mu4", "w0", "a0", "k_k", "k_a", "r_k", "lnx_g", "lnx_b", "misc", "pad"])}
        vv = lambda name, n=512: vecB[:, VEC[name] * 512:VEC[name] * 512 + n]
        SCALE = 0.125

        def rsqrt_ip(ap_fn, key):
            A("act", lambda e: e.activation(out=ap_fn(), in_=ap_fn(), func=AF.Sqrt), (key,), (key,))
            A("dve", lambda e: e.reciprocal(out=ap_fn(), in_=ap_fn()), (key,), (key,))

        for l in range(2):
            li = 0.8 - 0.6 * math.exp(-0.3 * l)
            Dm("sp", vecB[:], vec[l][:, 0:15 * 512].partition_broadcast(128), (), ("vecB",), "c1")
            Dm("sp", lamB[:], lamv[l].partition_broadcast(128), (), ("lamB",), "c1")
            Dm("sp", junk[0:64, 0:512], w2[l], (), ("junk",), "c1")
            Dm("sp", junk[64:128, 0:512], a2[l], ("junk",), ("junk",), "c1")
            A("dve", lambda e: e.tensor_copy(out=w2a2b[:], in_=junk[:, 0:512]), ("junk",), ("w2a2b",))
            A("dve", lambda e: e.tensor_tensor(out=lamB[:, 0:128], in0=lamB[:, 0:128], in1=lamB[:, 128:256], op=ALU.mult), ("lamB",), ("lamB",))
            A("dve", lambda e: e.tensor_reduce(out=lamT[:, 0:2], in_=lamB[:, 0:128].rearrange("p (a b) -> p a b", a=2), axis=AX.X, op=ALU.add), ("lamB",), ("lamT",))
            A("act", lambda e: e.activation(out=lamT[:, 2:4], in_=lamT[:, 0:2], func=AF.Exp), ("lamT",), ("lamT",))
            A("dve", lambda e: e.tensor_tensor(out=lam[:], in0=lamT[:, 2:3], in1=lamT[:, 3:4], op=ALU.subtract), ("lamT",), ("lam",))
            A("dve", lambda e, li=li: e.tensor_scalar(out=lamn[:], in0=lam[:], scalar1=li, scalar2=-1.0, op0=ALU.add, op1=ALU.mult), ("lam",), ("lamn",))
            A("pool", lambda e: e.memset(Mst[:], 0.0), (), ("Mst",))
            A("pool", lambda e: e.memset(Mb[:], 0.0), (), ("Mb",))
            A("pool", lambda e: e.memset(xsft[:], 0.0), (), ("xsft",))

            def tile_body(l, ti, li):
                samp = (ti == NT - 1)
                xk = "xs%d" % ti
                Dm("sp", xt[:], xs[ti * 128:(ti + 1) * 128, :], (xk,), ("xt",), "xin")
                A("act", lambda e: e.activation(out=junk[:], in_=xt[:], func=AF.Square, accum_out=ssq[:]), ("xt",), ("junk", "ssq"))
                A("dve", lambda e: e.tensor_scalar(out=rstd[:], in0=ssq[:], scalar1=1.0 / D, scalar2=1e-6, op0=ALU.mult, op1=ALU.add), ("ssq",), ("rstd",))
                rsqrt_ip(lambda: rstd[:], "rstd")
                if not samp:
                    pass
                A("dve", lambda e: e.scalar_tensor_tensor(out=xn[:, 0:512], in0=xt[:, 0:512], scalar=rstd[:, 0:1], in1=vecB[:, 0:512], op0=ALU.mult, op1=ALU.mult), ("xt", "rstd", "vecB"), ("xn",))
                A("dve", lambda e: e.scalar_tensor_tensor(out=xn[:, 512:1024], in0=xt[:, 512:1024], scalar=rstd[:, 0:1], in1=vecB[:, 512:1024], op0=ALU.mult, op1=ALU.mult), ("xt", "rstd", "vecB"), ("xn",))
                if not samp:
                    Dm("sp", xsft[1:128, :], xn[0:127, :], ("xn",), ("xsft",), "sft")
                    if ti == NPT - 1:
                        Dm("sp", hp_o[l], xn[127:128, :], ("xn",), (), "out")
                else:
                    Dm("sp", xsft[1:128, :], xn[0:127, :], ("xn",), ("xsft",), "sft")
                    Dm("sp", xsft[0:128:8, :], ssh[l], ("xsft",), ("xsft",), "sft")
                    Dm("sp", hs_o[l], xn[7:128:8, :], ("xn",), (), "out")
                transpose_to(lambda j: xnT[:, j, :], lambda j: xn[:, j * 128:(j + 1) * 128], 8, ("xn",), "xnT", dst4=lambda j0, nn: xnT[:, j0:j0 + nn, :])
                transpose_to(lambda j: xsT[:, j, :], lambda j: xsft[:, j * 128:(j + 1) * 128], 8, ("xsft",), "xsT", dst4=lambda j0, nn: xsT[:, j0:j0 + nn, :])
                if not samp:
                    Dm("sp", xsft[0:1, :], xn[127:128, :], ("xn", "xsT"), ("xsft",), "sft")
                if STAGE <= 0:
                    return
                linear(hrw, lambda k: xnT[:, k, :], 8, (wq_in[l], "wq_in%d" % l), 0, RWC, ("xnT",), "hrw", "p")
                for c0 in range(0, RWC, 1024):
                    cw = min(1024, RWC - c0)
                    linear(mm, lambda k: xsT[:, k, :], 8, (wq_in[l], "wq_in%d" % l), c0, cw, ("xsT",), "mm", "p")
                    A("dve", lambda e, c0=c0, cw=cw: e.tensor_tensor(out=mm[:, 0:cw], in0=mm[:, 0:cw], in1=hrw[:, c0:c0 + cw], op=ALU.subtract), ("mm", "hrw"), ("mm",))
                    A("dve", lambda e, c0=c0, cw=cw: e.tensor_tensor(out=mm[:, 0:cw], in0=mm[:, 0:cw], in1=vecB[:, 1024 + c0:1024 + c0 + cw], op=ALU.mult), ("mm", "vecB"), ("mm",))
                    A("dve", lambda e, c0=c0, cw=cw: e.tensor_tensor(out=hrw[:, c0:c0 + cw], in0=hrw[:, c0:c0 + cw], in1=mm[:, 0:cw], op=ALU.add), ("mm", "hrw"), ("hrw",))
                PQ = lambda c, n: prj[:, c - RWC:c - RWC + n]
                if STAGE <= 1:
                    return
                A("pe", lambda e: e.transpose(PS[5][:, 0:128], hrw[:, C_WD:C_WD + 128], ident[:]), ("hrw", "ident"), ("ps5",))
                A("act", lambda e: e.activation(out=wdT[0:64, :], in_=PS[5][0:64, 0:128], func=AF.Tanh), ("ps5",), ("wdT",))
                A("act", lambda e: e.activation(out=wdT[64:128, :], in_=PS[5][64:128, 0:128], func=AF.Copy), ("ps5",), ("wdT",))
                A("pe", lambda e: e.matmul(PS[5][:, :], lhsT=wdT[0:64, :], rhs=w2a2b[0:64, :], start=True, stop=True), ("wdT", "w2a2b"), ("ps5",))
                A("pe", lambda e: e.matmul(PS[7][:, :], lhsT=wdT[64:128, :], rhs=w2a2b[64:128, :], start=True, stop=True), ("wdT", "w2a2b"), ("ps7",))
                A("dve", lambda e: e.tensor_tensor(out=wt[:], in0=PS[5][:, :], in1=vv("w0"), op=ALU.add), ("ps5", "vecB"), ("wt",))
                A("dve", lambda e: e.tensor_tensor(out=at[:], in0=PS[7][:, :], in1=vv("a0"), op=ALU.add), ("ps7", "vecB"), ("at",))
                A("act", lambda e: e.activation(out=wt[:], in_=wt[:], func=AF.Sigmoid), ("wt",), ("wt",))
                A("act", lambda e: e.activation(out=at[:], in_=at[:], func=AF.Sigmoid), ("at",), ("at",))
                A("dve", lambda e: e.tensor_single_scalar(out=wt[:], in_=wt[:], scalar=-math.exp(-0.5), op=ALU.mult), ("wt",), ("wt",))
                A("dve", lambda e: e.tensor_tensor(out=kkt[:], in0=hrw[:, C_K:C_K + 512], in1=vv("k_k"), op=ALU.mult), ("hrw", "vecB"), ("kkt",))
                A("dve", lambda e: e.tensor_tensor(out=t512[:], in0=kkt[:], in1=kkt[:], op=ALU.mult), ("kkt",), ("t512",))
                A("dve", lambda e: e.tensor_reduce(out=s8[:], in_=t512[:].rearrange("p (h j) -> p h j", h=8), axis=AX.X, op=ALU.add), ("t512",), ("s8",))
                A("dve", lambda e: e.tensor_single_scalar(out=s8[:], in_=s8[:], scalar=1e-24, op=ALU.max), ("s8",), ("s8",))
                rsqrt_ip(lambda: s8[:], "s8")
                A("dve", lambda e: e.tensor_tensor(out=kkt[:].rearrange("p (h j) -> p h j", h=8), in0=kkt[:].rearrange("p (h j) -> p h j", h=8), in1=s8[:].unsqueeze(2).to_broadcast([128, 8, 64]), op=ALU.mult), ("kkt", "s8"), ("kkt",))
                A("dve", lambda e: e.tensor_tensor(out=bt[:], in0=kkt[:], in1=at[:], op=ALU.mult), ("kkt", "at"), ("bt",))
                A("dve", lambda e: e.scalar_tensor_tensor(out=t512[:], in0=at[:], scalar=-1.0, in1=vv("k_a"), op0=ALU.add, op1=ALU.mult), ("at", "vecB"), ("t512",))
                A("dve", lambda e: e.scalar_tensor_tensor(out=kmt[:], in0=t512[:], scalar=1.0, in1=hrw[:, C_K:C_K + 512], op0=ALU.add, op1=ALU.mult), ("t512", "hrw"), ("kmt",))
                if STAGE <= 2:
                    return
                CSi = 1 if samp else 0
                CS = 8 if samp else 16
                NCHT = 128 // CS
                TRIi = cmk[:, CSi, :]; TRIs = cmk[:, 2 + CSi, :]; LOWs = cmk[:, 4 + CSi, :]; BLK = cmk[:, 6 + CSi, :]
                x32 = junk[:, 0:512]
                A("pe", lambda e: e.matmul(PS[0][:, :], lhsT=TRIi, rhs=wt[:], start=True, stop=True), ("cmk", "wt"), ("ps0",))
                A("pe", lambda e: e.matmul(PS[1][:, :], lhsT=BLK, rhs=wt[:], start=True, stop=True), ("cmk", "wt"), ("ps1",))
                A("act", lambda e: e.activation(out=cwt[:], in_=PS[0][:, :], func=AF.Copy), ("ps0",), ("sc",))
                A("pool", lambda e: e.tensor_copy(out=Vb[:], in_=hrw[:, C_V:C_V + 512]), ("hrw",), ("Vb",))

                if KC <= 1:
                    return
                def tr8(dstT):
                    for g in range(2):
                        for k in range(4):
                            A("pe", lambda e, g=g, k=k: e.transpose(PS[5 + g][0:64, k * 128:(k + 1) * 128], junk[:, (g * 4 + k) * 64:(g * 4 + k + 1) * 64], ident[:]), ("junk", "ident"), ("ps%d" % (5 + g),))
                        A("act", lambda e, g=g: e.activation(out=dstT[:, g * 4:(g + 1) * 4, :], in_=PS[5 + g][0:64, :].rearrange("p (k t) -> p k t", k=4), func=AF.Copy), ("ps%d" % (5 + g),), ("xT",))
                A("act", lambda e: e.activation(out=t512[:], in_=PS[0][:, :], func=AF.Exp), ("ps0",), ("t512",))
                A("dve", lambda e: e.tensor_tensor(out=x32, in0=hrw[:, C_R:C_R + 512], in1=t512[:], op=ALU.mult), ("hrw", "t512"), ("junk",))
                A("pool", lambda e: e.tensor_copy(out=Rpb[:], in_=x32), ("junk",), ("Rpb",))
                tr8(RpT)
                A("dve", lambda e: e.tensor_tensor(out=t512[:], in0=cwt[:], in1=wt[:], op=ALU.subtract), ("sc", "wt"), ("t512",))
                A("act", lambda e: e.activation(out=t512[:], in_=t512[:], func=AF.Exp), ("t512",), ("t512",))
                A("dve", lambda e: e.tensor_tensor(out=x32, in0=kkt[:], in1=t512[:], op=ALU.mult), ("kkt", "t512"), ("junk",))
                A("pool", lambda e: e.tensor_copy(out=Pmb[:], in_=x32), ("junk",), ("Pmb",))
                tr8(PmT)
                A("act", lambda e: e.activation(out=t512[:], in_=cwt[:], func=AF.Exp, scale=-1.0), ("sc",), ("t512",))
                A("dve", lambda e: e.tensor_tensor(out=x32, in0=bt[:], in1=t512[:], op=ALU.mult), ("bt", "t512"), ("junk",))
                tr8(BnT)
                A("dve", lambda e: e.tensor_tensor(out=x32, in0=kmt[:], in1=t512[:], op=ALU.mult), ("kmt", "t512"), ("junk",))
                tr8(KnT)
                A("dve", lambda e: e.tensor_tensor(out=t512[:], in0=PS[1][:, :], in1=cwt[:], op=ALU.subtract), ("ps1", "sc"), ("t512",))
                A("act", lambda e: e.activation(out=t512[:], in_=t512[:], func=AF.Exp), ("t512",), ("t512",))
                A("dve", lambda e: e.tensor_tensor(out=BCb[:], in0=bt[:], in1=t512[:], op=ALU.mult), ("bt", "t512"), ("BCb",))
                A("dve", lambda e: e.tensor_tensor(out=KCb[:], in0=kmt[:], in1=t512[:], op=ALU.mult), ("kmt", "t512"), ("KCb",))
                A("act", lambda e: e.activation(out=x32, in_=PS[1][:, :], func=AF.Exp), ("ps1", "junk"), ("junk",))
                for g in range(2):
                    for k in range(4):
                        A("pe", lambda e, g=g, k=k: e.transpose(PS[5 + g][0:64, k * 128:(k + 1) * 128], junk[:, (g * 4 + k) * 64:(g * 4 + k + 1) * 64], ident[:]), ("junk", "ident"), ("ps%d" % (5 + g),))
                    A("act", lambda e, g=g: e.activation(out=GamT[:, g * 4:(g + 1) * 4, 0:NCHT], in_=PS[5 + g][0:64, :].rearrange("p (k t) -> p k t", k=4)[:, :, 0:128:CS], func=AF.Copy), ("ps%d" % (5 + g),), ("GamT",))
                if KC <= 2:
                    return
                hs = lambda h: slice(h * 64, (h + 1) * 64)
                def head_gen(h, par):
                    sp_ = "_%d" % par
                    B2, B3, B4 = ((2, 3, 4), (0, 1, 6))[par]
                    XK = ("xT",)
                    A("pe", lambda e, h=h: e.matmul(PS[B2][:, 0:128], lhsT=BnT[:, h, :], rhs=PmT[:, h, :], start=True, stop=True), XK, ("ps%d" % B2,))
                    A("pe", lambda e, h=h: e.matmul(PS[B2][:, 128:256], lhsT=BnT[:, h, :], rhs=RpT[:, h, :], start=True, stop=True), XK, ("ps%d" % B2,))
                    A("pe", lambda e, h=h: e.matmul(PS[B2][:, 256:384], lhsT=KnT[:, h, :], rhs=PmT[:, h, :], start=True, stop=True), XK, ("ps%d" % B2,))
                    A("pe", lambda e, h=h: e.matmul(PS[B2][:, 384:512], lhsT=KnT[:, h, :], rhs=RpT[:, h, :], start=True, stop=True), XK, ("ps%d" % B2,))
                    A("pe", lambda e, h=h: e.matmul(PS[B3][:, 0:128], lhsT=PmT[:, h, :], rhs=BnT[:, h, :], start=True, stop=True), XK, ("ps%d" % B3,))
                    P2 = lambda: PS[B2][:, :].rearrange("p (a t) -> p a t", a=4)
                    A("dve", lambda e: e.tensor_tensor(out=MA_[par][:, 0:4:2, :], in0=P2()[:, 0:4:2, :], in1=TRIs.unsqueeze(1).to_broadcast([128, 2, 128]), op=ALU.mult), ("ps%d" % B2, "cmk"), ("MA" + sp_,))
                    yield
                    A("dve", lambda e: e.tensor_tensor(out=MA_[par][:, 1:4:2, :], in0=P2()[:, 1:4:2, :], in1=TRIi.unsqueeze(1).to_broadcast([128, 2, 128]), op=ALU.mult), ("ps%d" % B2, "cmk"), ("MA" + sp_,))
                    yield
                    A("dve", lambda e: e.tensor_tensor(out=DB_[par][:, 0, :], in0=PS[B3][:, 0:128], in1=LOWs, op=ALU.mult), ("ps%d" % B3, "cmk"), ("DB0" + sp_,))
                    yield
                    A("pool", lambda e: e.tensor_tensor(out=DB_[par][:, 1, :], in0=identb[:], in1=MA_[par][:, 0, :], op=ALU.subtract), ("identb", "MA" + sp_), ("DB1" + sp_,))
                    yield
                    A("pe", lambda e, h=h: e.matmul(PS[B3][:, 128:192], lhsT=MA_[par][:, 2, :], rhs=Vb[:, hs(h)], start=True, stop=True), ("MA" + sp_, "Vb"), ("ps%d" % B3,))
                    A("act", lambda e: e.activation(out=RR_[par][:, 0, 64:128], in_=PS[B3][:, 128:192], func=AF.Copy), ("ps%d" % B3,), ("RR0" + sp_,))
                    yield
                    A("pool", lambda e, h=h: e.tensor_copy(out=RR_[par][:, 0, 0:64], in_=Pmb[:, hs(h)]), ("Pmb", "RR0" + sp_), ("RR0" + sp_,))
                    yield
                    A("pe", lambda e: e.matmul(PS[B4][:, 0:128], lhsT=DB_[par][:, 0, :], rhs=MA_[par][:, 0, :], start=True, stop=True), ("DB0" + sp_, "MA" + sp_), ("ps%d" % B4,))
                    A("pe", lambda e: e.matmul(PS[B4][:, 128:256], lhsT=MA_[par][:, 0, :], rhs=DB_[par][:, 0, :], start=True, stop=True), ("DB0" + sp_, "MA" + sp_), ("ps%d" % B4,))
                    A("act", lambda e: e.activation(out=DB_[par][:, 2:4, :], in_=PS[B4][:, 0:256].rearrange("p (a t) -> p a t", a=2), func=AF.Copy), ("ps%d" % B4,), ("DB23" + sp_,))
                    yield
                    A("pool", lambda e: e.tensor_tensor(out=DB_[par][:, 4, :], in0=DB_[par][:, 2, :], in1=identb[:], op=ALU.add), ("DB23" + sp_, "identb"), ("DB4" + sp_,))
                    yield
                    A("pe", lambda e: e.matmul(PS[B4][:, 256:384], lhsT=DB_[par][:, 3, :], rhs=DB_[par][:, 2, :], start=True, stop=True), ("DB23" + sp_,), ("ps%d" % B4,))
                    A("pe", lambda e: e.matmul(PS[B4][:, 384:512], lhsT=DB_[par][:, 2, :], rhs=DB_[par][:, 3, :], start=True, stop=True), ("DB23" + sp_,), ("ps%d" % B4,))
                    A("act", lambda e: e.activation(out=DB_[par][:, 5:7, :], in_=PS[B4][:, 256:512].rearrange("p (a t) -> p a t", a=2), func=AF.Copy), ("ps%d" % B4,), ("DB56" + sp_,))
                    yield
                    A("pool", lambda e: e.tensor_tensor(out=DB_[par][:, 7, :], in0=DB_[par][:, 5, :], in1=identb[:], op=ALU.add), ("DB56" + sp_, "identb"), ("DB7" + sp_,))
                    yield
                    if CS == 16:
                        A("pe", lambda e: e.matmul(PS[B3][:, 256:384], lhsT=DB_[par][:, 6, :], rhs=DB_[par][:, 5, :], start=True, stop=True), ("DB56" + sp_,), ("ps%d" % B3,))
                        A("dve", lambda e: e.tensor_tensor(out=DB_[par][:, 8, :], in0=PS[B3][:, 256:384], in1=ident[:], op=ALU.add), ("ps%d" % B3, "ident"), ("DB8" + sp_,))
                        yield
                    lv = [(1, "DB1" + sp_), (4, "DB4" + sp_), (7, "DB7" + sp_)] + ([(8, "DB8" + sp_)] if CS == 16 else [])
                    for k, (di, dk) in enumerate(lv):
                        A("pe", lambda e, k=k, di=di: e.matmul(PS[7][:, k * 128:(k + 1) * 128], lhsT=DB_[par][:, di, :], rhs=RR_[par][:, k, :], start=True, stop=True), (dk, ("RR%d" % k) + sp_), ("ps7",))
                        if k < len(lv) - 1:
                            A("act", lambda e, k=k: e.activation(out=RR_[par][:, k + 1, :], in_=PS[7][:, k * 128:(k + 1) * 128], func=AF.Copy), ("ps7",), (("RR%d" % (k + 1)) + sp_,))
                            yield
                        else:
                            A("act", lambda e, k=k, h=h: e.activation(out=PUn[:, h, :], in_=PS[7][:, k * 128:(k + 1) * 128], func=AF.Copy, scale=-1.0), ("ps7",), ("PUn",))
                            yield
                    A("pe", lambda e, h=h: e.matmul(PS[B3][0:64, 384:512], lhsT=Rpb[:, hs(h)], rhs=identb[:], start=True, stop=False), ("Rpb", "identb"), ("ps%d" % B3,))
                    A("pe", lambda e, h=h: e.matmul(PS[B3][0:64, 384:512], lhsT=PUn[:, h, 0:64], rhs=MA_[par][:, 1, :], start=False, stop=True), ("PUn", "MA" + sp_), ("ps%d" % B3,))
                    A("act", lambda e, h=h: e.activation(out=RhT[:, h, :], in_=PS[B3][0:64, 384:512], func=AF.Copy), ("ps%d" % B3,), ("RhT",))
                    yield
                    A("pe", lambda e, h=h: e.matmul(PS[5][0:64, par * 128:(par + 1) * 128], lhsT=Vb[:, hs(h)], rhs=MA_[par][:, 3, :], start=True, stop=False), ("Vb", "MA" + sp_), ("ps5",))
                    A("pe", lambda e, h=h: e.matmul(PS[5][0:64, par * 128:(par + 1) * 128], lhsT=PUn[:, h, 64:128], rhs=MA_[par][:, 1, :], start=False, stop=True), ("PUn", "MA" + sp_), ("ps5",))
                    A("act", lambda e, h=h: e.activation(out=YL[:, h, :], in_=PS[5][0:64, par * 128:(par + 1) * 128], func=AF.Copy), ("ps5",), ("osb",))
                    yield
                for hp_ in range(4):
                    gens = [head_gen(2 * hp_, 0), head_gen(2 * hp_ + 1, 1)]
                    while gens:
                        for g_ in list(gens):
                            try:
                                next(g_)
                            except StopIteration:
                                gens.remove(g_)

                if KC <= 3:
                    return
                YTP = lambda h: PS[h // 4][0:64, (h % 4) * 128:(h % 4 + 1) * 128]
                for p in range(2 if samp else 1):
                    CM = cmc[:, (1 + p) if samp else 0, :]
                    for h in range(8):
                        A("pool", lambda e, h=h, CM=CM: e.tensor_tensor(out=BCm[:], in0=BCb[:, hs(h)].unsqueeze(1).to_broadcast([128, 8, 64]), in1=CM.unsqueeze(2).to_broadcast([128, 8, 64]), op=ALU.mult), ("BCb", "cmc"), ("BCm",))
                        A("pe", lambda e, h=h: e.matmul(PS[6][0:64, :], lhsT=PUn[:, h, 0:64], rhs=BCm[:].rearrange("p c j -> p (c j)"), start=True, stop=True), ("PUn", "BCm"), ("ps6",))
                        A("act", lambda e, h=h: e.activation(out=GT[:, h, :, :], in_=PS[6][0:64, :].rearrange("p (c j) -> p c j", c=8), func=AF.Copy), ("ps6",), ("prj",))
                        A("pool", lambda e, h=h, CM=CM: e.tensor_tensor(out=Vm[:], in0=Vb[:, hs(h)].unsqueeze(1).to_broadcast([128, 8, 64]), in1=CM.unsqueeze(2).to_broadcast([128, 8, 64]), op=ALU.mult), ("Vb", "cmc"), ("pT",))
                        A("dve", lambda e, h=h, CM=CM: e.tensor_tensor(out=UVm[:], in0=PUn[:, h, 64:128].unsqueeze(1).to_broadcast([128, 8, 64]), in1=CM.unsqueeze(2).to_broadcast([128, 8, 64]), op=ALU.mult), ("PUn", "cmc"), ("pTb",))
                        A("pe", lambda e, h=h: e.matmul(PS[7][0:64, :], lhsT=KCb[:, hs(h)], rhs=Vm[:].rearrange("p c j -> p (c j)"), start=True, stop=False), ("KCb", "pT"), ("ps7",))
                        A("pe", lambda e, h=h: e.matmul(PS[7][0:64, :], lhsT=BCb[:, hs(h)], rhs=UVm[:].rearrange("p c j -> p (c j)"), start=False, stop=True), ("BCb", "pTb"), ("ps7",))
                        A("act", lambda e, h=h: e.activation(out=HS[:, h, :, :], in_=PS[7][0:64, :].rearrange("p (c j) -> p c j", c=8), func=AF.Copy), ("ps7",), ("prj",))
                    for c in range(8):
                        cidx = 8 * p + c
                        if samp:
                            Dm("sp", Sv[:], srw[l, cidx].rearrange("h i j -> i h j"), (), ("Sv",), "sld")
                            for h in range(8):
                                A("pe", lambda e, h=h: e.transpose(PS[4][0:64, hs(h)], Sv[:, h, :], ident[0:64, 0:64]), ("Sv", "ident"), ("ps4",))
                            A("dve", lambda e: e.tensor_copy(out=Mst[:], in_=PS[4][0:64, :].rearrange("p (h i) -> p h i", h=8)), ("ps4",), ("Mst",))
                            A("act", lambda e: e.activation(out=Mb[:], in_=Mst[:], func=AF.Copy), ("Mst",), ("Mb",))
                        pc = 4 if (c % 2 == 0) else 6
                        for h in range(8):
                            A("pe", lambda e, h=h, c=c, pc=pc: e.matmul(PS[pc][0:64, hs(h)], lhsT=GT[:, h, c, :], rhs=Mb[:, h, :], start=True, stop=False), ("prj", "Mb"), ("ps%d" % pc,))
                            A("pe", lambda e, h=h, c=c, pc=pc: e.matmul(PS[pc][0:64, hs(h)], lhsT=identb[0:64, 0:64], rhs=HS[:, h, c, :], start=False, stop=True), ("prj", "identb"), ("ps%d" % pc,))
                        for h in range(8):
                            A("pe", lambda e, h=h, cidx=cidx: e.matmul(YTP(h)[:, cidx * CS:(cidx + 1) * CS], lhsT=Mb[:, h, :], rhs=RhT[:, h, cidx * CS:(cidx + 1) * CS], start=True, stop=True), ("Mb", "RhT"), ("ps%d" % (h // 4),))
                        A("dve", lambda e, cidx=cidx: e.tensor_tensor(out=Mst[:], in0=Mst[:], in1=GamT[:, :, cidx:cidx + 1].to_broadcast([64, 8, 64]), op=ALU.mult), ("Mst", "GamT"), ("Mst",))
                        if not samp:
                            A("dve", lambda e, pc=pc: e.tensor_tensor(out=Mb[:], in0=Mst[:], in1=PS[pc][0:64, :].rearrange("p (h i) -> p h i", h=8), op=ALU.add), ("Mst", "ps%d" % pc), ("Mb",))
                        A("dve", lambda e, pc=pc: e.tensor_tensor(out=Mst[:], in0=Mst[:], in1=PS[pc][0:64, :].rearrange("p (h i) -> p h i", h=8), op=ALU.add), ("Mst", "ps%d" % pc), ("Mst",))
                        last_state = samp or (ti == NPT - 1 and c == 7)
                        if last_state:
                            for h in range(8):
                                A("pe", lambda e, h=h: e.transpose(PS[4][0:64, hs(h)], Mst[:, h, :], ident[0:64, 0:64]), ("Mst", "ident"), ("ps4",))
                            A("act", lambda e: e.activation(out=Sv[:], in_=PS[4][0:64, :].rearrange("p (h j) -> p h j", h=8), func=AF.Copy), ("ps4",), ("Sv",))
                            dst = ss_o[l, cidx] if samp else sp_o[l]
                            Dm("sp", dst.rearrange("h i j -> i h j"), Sv[:], ("Sv",), (), "out")
                if KC <= 5:
                    return
                A("dve", lambda e: e.tensor_tensor(out=YL[:, 0:4, :], in0=YL[:, 0:4, :], in1=PS[0][0:64, :].rearrange("p (h t) -> p h t", h=4), op=ALU.add), ("osb", "ps0"), ("osb",))
                A("dve", lambda e: e.tensor_tensor(out=YL[:, 4:8, :], in0=YL[:, 4:8, :], in1=PS[1][0:64, :].rearrange("p (h t) -> p h t", h=4), op=ALU.add), ("osb", "ps1"), ("osb",))
                if STAGE <= 3:
                    return
                for h in range(8):
                    A("pe", lambda e, h=h: e.transpose(PS[5][:, hs(h)], YL[:, h, :], ident[0:64, 0:64]), ("osb", "ident"), ("ps5",))
                A("act", lambda e: e.activation(out=ya[:], in_=PS[5][:, :], func=AF.Copy), ("ps5",), ("ya",))
                if (not samp) and ti == 0:
                    A("pool", lambda e: e.memset(ya[0:96, :], 0.0), ("ya",), ("ya",))
                linear(prj, lambda k: xnT[:, k, :], 8, (wq_in[l], "wq_in%d" % l), RWC, NCOLS - RWC, ("xnT",), "prj", "p")
                Y3 = lambda tl: tl[:].rearrange("p (h j) -> p h j", h=8)
                A("dve", lambda e: e.tensor_reduce(out=s8[:], in_=Y3(ya), axis=AX.X, op=ALU.add), ("ya",), ("s8",))
                A("dve", lambda e: e.tensor_single_scalar(out=s8[:], in_=s8[:], scalar=-1.0 / 64, op=ALU.mult), ("s8",), ("s8",))
                A("dve", lambda e: e.tensor_tensor(out=Y3(ya), in0=Y3(ya), in1=s8[:].unsqueeze(2).to_broadcast([128, 8, 64]), op=ALU.add), ("ya", "s8"), ("ya",))
                A("dve", lambda e: e.tensor_tensor(out=t512[:], in0=ya[:], in1=ya[:], op=ALU.mult), ("ya",), ("t512",))
                A("dve", lambda e: e.tensor_reduce(out=s8b[:], in_=Y3(t512), axis=AX.X, op=ALU.add), ("t512",), ("s8b",))
                A("dve", lambda e: e.tensor_scalar(out=s8b[:], in0=s8b[:], scalar1=1.0 / 64, scalar2=64e-5, op0=ALU.mult, op1=ALU.add), ("s8b",), ("s8b",))
                rsqrt_ip(lambda: s8b[:], "s8b")
                A("dve", lambda e: e.tensor_tensor(out=Y3(ya), in0=Y3(ya), in1=s8b[:].unsqueeze(2).to_broadcast([128, 8, 64]), op=ALU.mult), ("ya", "s8b"), ("ya",))
                A("dve", lambda e: e.tensor_tensor(out=ya[:], in0=ya[:], in1=vv("lnx_g"), op=ALU.mult), ("ya", "vecB"), ("ya",))
                A("dve", lambda e: e.tensor_tensor(out=ya[:], in0=ya[:], in1=vv("lnx_b"), op=ALU.add), ("ya", "vecB"), ("ya",))
                A("dve", lambda e: e.tensor_tensor(out=t512[:], in0=hrw[:, C_R:C_R + 512], in1=kmt[:], op=ALU.mult), ("hrw", "kmt"), ("t512",))
                A("dve", lambda e: e.tensor_tensor(out=t512[:], in0=t512[:], in1=vv("r_k"), op=ALU.mult), ("t512", "vecB"), ("t512",))
                A("dve", lambda e: e.tensor_reduce(out=s8[:], in_=Y3(t512), axis=AX.X, op=ALU.add), ("t512",), ("s8",))
                A("dve", lambda e: e.tensor_tensor(out=Y3(t512), in0=hrw[:, C_V:C_V + 512].rearrange("p (h j) -> p h j", h=8), in1=s8[:].unsqueeze(2).to_broadcast([128, 8, 64]), op=ALU.mult), ("hrw", "s8"), ("t512",))
                A("dve", lambda e: e.tensor_tensor(out=ya[:], in0=ya[:], in1=t512[:], op=ALU.add), ("ya", "t512"), ("ya",))
                A("act", lambda e: e.activation(out=t512[:], in_=hrw[:, C_Z:C_Z + 512], func=AF.Silu), ("hrw",), ("t512",))
                A("dve", lambda e: e.tensor_tensor(out=ya[:], in0=ya[:], in1=t512[:], op=ALU.mult), ("ya", "t512"), ("ya",))

                if STAGE <= 4:
                    return
                for (c0, gname) in ((C_Q, "qg"), (C_KK, "kg")):
                    src = PQ(c0, 512)
                    A("dve", lambda e, src=src: e.tensor_tensor(out=t512[:], in0=src, in1=src, op=ALU.mult), ("prj",), ("t512",))
                    A("dve", lambda e: e.tensor_reduce(out=s8[:], in_=Y3(t512), axis=AX.X, op=ALU.add), ("t512",), ("s8",))
                    A("dve", lambda e: e.tensor_scalar(out=s8[:], in0=s8[:], scalar1=1.0 / 64, scalar2=1e-6, op0=ALU.mult, op1=ALU.add), ("s8",), ("s8",))
                    rsqrt_ip(lambda: s8[:], "s8")
                    A("dve", lambda e, src=src: e.tensor_tensor(out=src.rearrange("p (h j) -> p h j", h=8), in0=src.rearrange("p (h j) -> p h j", h=8), in1=s8[:].unsqueeze(2).to_broadcast([128, 8, 64]), op=ALU.mult), ("prj", "s8"), ("prj",))
                    gsl = vecB[:, VEC["misc"] * 512 + (0 if gname == "qg" else 64):VEC["misc"] * 512 + (64 if gname == "qg" else 128)]
                    A("dve", lambda e, src=src, gsl=gsl: e.tensor_tensor(out=src.rearrange("p (h j) -> p h j", h=8), in0=src.rearrange("p (h j) -> p h j", h=8), in1=gsl.unsqueeze(1).to_broadcast([128, 8, 64]), op=ALU.mult), ("prj", "vecB"), ("prj",))
                Dm("sp", k_o[l, ti * 128:(ti + 1) * 128, :], PQ(C_KK, 512), ("prj",), (), "out")
                Dm("sp", v_o[l, ti * 128:(ti + 1) * 128, :], PQ(C_VV, 512), ("prj",), (), "out")
                transpose_to(lambda j: qT[:, j, :], lambda j: PQ(C_Q + j * 128, 128), 4, ("prj",), "qT", dst4=lambda j0, nn: qT[:, j0:j0 + nn, :])
                if not samp:
                    transpose_to(lambda j: KT[:, j, ti * 128:(ti + 1) * 128], lambda j: PQ(C_KK + j * 128, 128), 4, ("prj",), "KT", dst4=lambda j0, nn: KT[:, j0:j0 + nn, ti * 128:(ti + 1) * 128])
                    A("pool", lambda e: e.tensor_copy(out=Vx[:, ti, :, 0:128], in_=PQ(C_VV, 512).rearrange("p (h j) -> p h j", h=4)), ("prj",), ("Vx",))
                    if ti == 0:
                        A("dve", lambda e: e.tensor_single_scalar(out=Vx[:, 0, :, 128], in_=io[:, 0:1].to_broadcast([128, 4]), scalar=112.0, op=ALU.is_ge), ("Vx", "io"), ("Vx",))
                else:
                    transpose_to(lambda j: ksT[:, j, :], lambda j: PQ(C_KK + j * 128, 128), 4, ("prj",), "ksT", dst4=lambda j0, nn: ksT[:, j0:j0 + nn, :])
                    A("pool", lambda e: e.tensor_copy(out=vsx[:, :, 0:128], in_=PQ(C_VV, 512).rearrange("p (h j) -> p h j", h=4)), ("prj",), ("vsx",))

                def attn_block(kT_fn, v_fn, bidx, nq, q0, first, last, rk, nkrows=128):
                    if KX <= 0:
                        return
                    for h in range(4):
                        for m in range(2):
                            A("pe", lambda e, h=h, m=m: e.matmul(PS[2 + m][0:nkrows, h * 128:h * 128 + nq], lhsT=kT_fn(h, m), rhs=qT[m * 64:(m + 1) * 64, h, q0:q0 + nq], start=True, stop=True), tuple(rk) + ("qT",), ("ps%d" % (2 + m),))
                    if KX <= 1:
                        return
                    for m in range(2):
                        A("dve", lambda e, m=m: e.scalar_tensor_tensor(out=sc[0:nkrows, m:8:2, 0:nq], in0=PS[2 + m][0:nkrows, :].rearrange("p (h q) -> p h q", h=4)[:, :, 0:nq], scalar=SCALE, in1=bias[0:nkrows, :, bidx, 0:nq], op0=ALU.mult, op1=ALU.add), ("ps%d" % (2 + m), "bias"), ("sc",))
                    if KX <= 2:
                        return
                    A("act", lambda e: e.activation(out=pT[0:nkrows, :, 0:nq], in_=sc[0:nkrows, :, 0:nq], func=AF.Exp), ("sc",), ("pT", "pTb"))
                    if KX <= 3:
                        return
                    BK = [0, 1, 7]
                    for h in range(4):
                        for m in range(2):
                            hm = h * 2 + m
                            A("pe", lambda e, h=h, hm=hm: e.matmul(PS[BK[hm // 3]][0:nq, (hm % 3) * 160:(hm % 3) * 160 + 130], lhsT=pT[0:nkrows, hm, 0:nq], rhs=v_fn(h)[:, 0:130], start=True, stop=True), ("pT", "pTb") + tuple(rk), ("ps%d" % BK[hm // 3],))
                    for bi in range(3):
                        n3 = 3 if bi < 2 else 2
                        src = lambda bi=bi, n3=n3: PS[BK[bi]][0:nq, 0:n3 * 160].rearrange("p (a b) -> p a b", b=160)[:, :, 0:129]
                        dst = lambda bi=bi, n3=n3: osb[0:nq, bi * 3:bi * 3 + n3, 0:129]
                        if first:
                            A("dve", lambda e, src=src, dst=dst: e.tensor_copy(out=dst(), in_=src()), ("ps%d" % BK[bi], "osb"), ("osb",))
                        else:
                            A("dve", lambda e, src=src, dst=dst: e.tensor_tensor(out=dst(), in0=dst(), in1=src(), op=ALU.add), ("ps%d" % BK[bi], "osb"), ("osb",))

                if KSUB == 0 or (KSUB == 1 and samp):
                    return
                if not samp:
                    nb = ti + 1
                    for j in range(nb):
                        dlt = ti - j
                        bidx = 0 if dlt == 0 else (1 if dlt == 1 else 2)
                        attn_block(lambda h, m, j=j: KT[m * 64:(m + 1) * 64, h, j * 128:(j + 1) * 128], lambda h, j=j: Vx[:, j, h, :], bidx, 128, 0, j == 0, j == nb - 1, ("KT", "Vx"))
                    A("dve", lambda e: e.tensor_single_scalar(out=rl[:], in_=osb[:, :, 128], scalar=1e-30, op=ALU.max), ("osb",), ("rl",))
                    A("dve", lambda e: e.reciprocal(out=rl[:], in_=rl[:]), ("rl",), ("rl",))
                else:
                    A("pool", lambda e: e.memset(vpxB[:, :, 128:130], 1.0), ("PUn",), ("PUn",))
                    SETS = [dict(kp=wt[:], kk=("wt",), vp=at[:], vk=("at",), kT=kpT, kTk="kpT", vx=vpx, vxk="vpx", sk="A"),
                            dict(kp=kpgB, kk=("Vb", "Rpb"), vp=vpgB, vk=("Pmb", "BCb"), kT=kpTB, kTk="KCb", vx=vpxB, vxk="PUn", sk="B")]
                    pgc = 0
                    for sq in range(16):
                        for pg in range(NPG):
                            col = sq * NPG + pg
                            S_ = SETS[pgc % 2]
                            pgc += 1
                            kb.add("pool", lambda e, col=col, S_=S_: e.indirect_dma_start(out=S_["kp"], out_offset=None, in_=ck, in_offset=bass.IndirectOffsetOnAxis(ap=(pidx1 if l else pidx)[:, col:col + 1], axis=0)), reads=("pidx",), writes=S_["kk"], dma=True, semkey="gk" + S_["sk"])
                            kb.add("pool", lambda e, col=col, S_=S_: e.indirect_dma_start(out=S_["vp"], out_offset=None, in_=cv, in_offset=bass.IndirectOffsetOnAxis(ap=(pidx1 if l else pidx)[:, col:col + 1], axis=0)), reads=("pidx",), writes=S_["vk"], dma=True, semkey="gv" + S_["sk"])
                            transpose_to(None, lambda j, S_=S_: S_["kp"][:, j * 128:(j + 1) * 128], 4, S_["kk"], S_["kTk"], ps_i=6, dst4=lambda j0, nn, S_=S_: S_["kT"][:, j0:j0 + nn, :])
                            A("act", lambda e, S_=S_: e.activation(out=S_["vx"][:, :, 0:128], in_=S_["vp"].rearrange("p (h j) -> p h j", h=4), func=AF.Copy), S_["vk"], (S_["vxk"],))
                            bidx = 3 if pg == NPG - 1 else 2
                            attn_block(lambda h, m, S_=S_: S_["kT"][m * 64:(m + 1) * 64, h, :], lambda h, S_=S_: S_["vx"][:, h, :], bidx, 8, sq * 8, pg == 0, False, (S_["kTk"], S_["vxk"]))
                        Dm("sp", vs8[:], vsx[sq * 8:sq * 8 + 8, :, :], ("vsx",), ("vs8",), "vs8")
                        attn_block(lambda h, m, sq=sq: ksT[m * 64:(m + 1) * 64, h, sq * 8:sq * 8 + 8], lambda h: vs8[:, h, :], 4, 8, sq * 8, False, True, ("ksT", "vs8"), nkrows=8)
                        A("dve", lambda e: e.reciprocal(out=rl[0:8, :], in_=osb[0:8, :, 128]), ("osb",), ("osb8",))
                        A("dve", lambda e: e.tensor_tensor(out=osb[0:8, :, 0:128], in0=osb[0:8, :, 0:128], in1=rl[0:8, :].unsqueeze(2).to_broadcast([8, 8, 128]), op=ALU.mult), ("osb8", "osb"), ("osb8", "osb"))
                        Dm("sp", xs[0:1, 0:1], xs[0:1, 0:1], (), (), "nopd") if False else None
                        A("dve", lambda e: e.scalar_tensor_tensor(out=osb[0:8, 0:8:2, 0:128], in0=osb[0:8, 1:8:2, 0:128], scalar=lamn[0:8, 0:1], in1=osb[0:8, 0:8:2, 0:128], op0=ALU.mult, op1=ALU.add), ("osb8", "osb", "lamn"), ("osb8", "osb"))
                        Dm("sp", o2[sq * 8:sq * 8 + 8, :].rearrange("p (h e) -> p h e", h=4), osb[0:8, 0:8:2, 0:128], ("osb8", "osb", "o2s"), ("o2s",), "o2d")
                if not samp:
                    A("dve", lambda e: e.tensor_tensor(out=osb[:, :, 0:128], in0=osb[:, :, 0:128], in1=rl[:].unsqueeze(2).to_broadcast([128, 8, 128]), op=ALU.mult), ("osb", "rl"), ("osb",))
                    A("dve", lambda e: e.scalar_tensor_tensor(out=o2[:].rearrange("p (h e) -> p h e", h=4), in0=osb[:, 1:8:2, 0:128], scalar=lamn[:, 0:1], in1=osb[:, 0:8:2, 0:128], op0=ALU.mult, op1=ALU.add), ("osb", "lamn"), ("o2",))
                o2k = ("o2s",) if samp else ("o2",)
                O4 = lambda tl: tl[:].rearrange("p (h e) -> p h e", h=4)
                A("dve", lambda e: e.tensor_tensor(out=t512[:], in0=o2[:], in1=o2[:], op=ALU.mult), o2k, ("t512",))
                A("dve", lambda e: e.tensor_reduce(out=s8[:, 0:4], in_=O4(t512), axis=AX.X, op=ALU.add), ("t512",), ("s8",))
                A("dve", lambda e: e.tensor_scalar(out=s8[:, 0:4], in0=s8[:, 0:4], scalar1=1.0 / 128, scalar2=1e-6, op0=ALU.mult, op1=ALU.add), ("s8",), ("s8",))
                rsqrt_ip(lambda: s8[:, 0:4], "s8")
                A("dve", lambda e: e.tensor_tensor(out=O4(yb), in0=O4(o2), in1=s8[:, 0:4].unsqueeze(2).to_broadcast([128, 4, 128]), op=ALU.mult), o2k + ("s8",), ("yb",))
                sg_ap = vecB[:, VEC["misc"] * 512 + 128:VEC["misc"] * 512 + 256]
                A("dve", lambda e: e.tensor_tensor(out=O4(yb), in0=O4(yb), in1=sg_ap.unsqueeze(1).to_broadcast([128, 4, 128]), op=ALU.mult), ("yb", "vecB"), ("yb",))
                A("act", lambda e: e.activation(out=t512[:], in_=PQ(C_ZB, 512), func=AF.Silu), ("prj",), ("t512",))
                A("dve", lambda e, li=li: e.scalar_tensor_tensor(out=yb[:], in0=yb[:], scalar=1.0 - li, in1=t512[:], op0=ALU.mult, op1=ALU.mult), ("yb", "t512"), ("yb",))
                if STAGE <= 5:
                    return
                transpose_to(lambda j: yT[:, j, :], lambda j: ya[:, j * 128:(j + 1) * 128], 4, ("ya",), "xsT", dst4=lambda j0, nn: yT[:, j0:j0 + nn, :])
                transpose_to(lambda j: yT[:, 4 + j, :], lambda j: yb[:, j * 128:(j + 1) * 128], 4, ("yb",), "xsT", dst4=lambda j0, nn: yT[:, 4 + j0:4 + j0 + nn, :])
                linear(mm, lambda k: yT[:, k, :], 4, (wq_a[l], "wq_a%d" % l), 0, D, ("xsT",), "mm", "p")
                A("act", lambda e: e.activation(out=junk[:], in_=PQ(C_GA, D), func=AF.Sigmoid), ("prj",), ("junk",))
                A("dve", lambda e: e.tensor_tensor(out=junk[:], in0=junk[:], in1=mm[:], op=ALU.mult), ("junk", "mm"), ("junk",))
                linear(mm, lambda k: yT[:, 4 + k, :], 4, (wq_b[l], "wq_b%d" % l), 0, D, ("xsT",), "mm", "p")
                A("act", lambda e: e.activation(out=xn[:], in_=PQ(C_GB, D), func=AF.Sigmoid), ("prj", "xnT"), ("xn",))
                A("dve", lambda e: e.tensor_tensor(out=mm[:], in0=mm[:], in1=xn[:], op=ALU.mult), ("xn", "mm"), ("mm",))
                A("dve", lambda e: e.tensor_tensor(out=mm[:], in0=mm[:], in1=junk[:], op=ALU.add), ("junk", "mm"), ("mm",))
                transpose_to(lambda j: mT[:, j, :], lambda j: mm[:, j * 128:(j + 1) * 128], 8, ("mm",), "xsT", dst4=lambda j0, nn: mT[:, j0:j0 + nn, :])
                linear(junk, lambda k: mT[:, k, :], 8, (wq_o[l], "wq_o%d" % l), 0, D, ("xsT",), "junk", "p")
                A("dve", lambda e: e.tensor_tensor(out=xt[:], in0=xt[:], in1=junk[:], op=ALU.add), ("xt", "junk"), ("xt",))
                if l == 0:
                    Dm("sp", xs[ti * 128:(ti + 1) * 128, :], xt[:], ("xt",), (xk,), "xso")
                else:
                    Dm("sp", y_o[ti * 128:(ti + 1) * 128, :], xt[:], ("xt",), (), "out")
            for ti in range(NT):
                tile_body(l, ti, li)
        kb.emit_all()
    return nc


_CACHE = {}


def _pack_vec(ins, l):
    v = np.zeros((16, 512), np.float32)
    f = lambda a: np.asarray(a, np.float32).reshape(-1)
    v[0] = f(ins["norm_g"][l])[:512]; v[1] = f(ins["norm_g"][l])[512:]
    mu = np.zeros(2560, np.float32); mu[:RWC] = f(ins["mu_shift"][l])
    v[2:7] = mu.reshape(5, 512)
    v[7] = f(ins["w0"][l]); v[8] = f(ins["a0"][l]); v[9] = f(ins["k_k"][l]); v[10] = f(ins["k_a"][l])
    v[11] = f(ins["r_k"][l]); v[12] = f(ins["lnx_g"][l]); v[13] = f(ins["lnx_b"][l])
    v[14, 0:64] = f(ins["q_norm_g"][l]); v[14, 64:128] = f(ins["k_norm_g"][l]); v[14, 128:256] = f(ins["subln_g"][l])
    return v.reshape(1, -1)


def kernel(**ins):
    ins = {k: np.asarray(v) for k, v in ins.items()}
    B, SEQ, _ = ins["x_prompt"].shape
    DB, DS, _ = ins["x_sample"].shape
    NPG = ins["page_table"].shape[1]
    NPOOL = ins["cache_k"].shape[1]
    NPT = SEQ // 128 + 1
    NT = NPT + 1
    ncores = B
    assert DB == 16 * ncores and DS == 8
    key = (NPT, NPG, NPOOL)
    if key not in _CACHE:
        _CACHE[key] = build(*key)
    nc = _CACHE[key]
    mats, msk = make_consts()
    cmk_np, cmc_np = make_chunk_consts()
    bm_in = np.ascontiguousarray(mats.transpose(1, 0, 2)); mk_in = np.ascontiguousarray(msk.transpose(1, 0, 2))
    ck = ins["cache_k"].reshape(2 * NPOOL * 128, 512); cv = ins["cache_v"].reshape(2 * NPOOL * 128, 512)
    vec = np.stack([_pack_vec(ins, l) for l in range(2)])
    lamv = np.stack([np.concatenate([ins["lam_q1"][l], ins["lam_q2"][l], ins["lam_k1"][l], ins["lam_k2"][l]]).reshape(1, 256) for l in range(2)]).astype(np.float32)
    in_maps = []
    for c in range(ncores):
        xin = np.zeros((NT * 128, D), np.float32)
        xin[112:128] = ins["meta_tokens"]
        xin[128:128 + SEQ] = ins["x_prompt"][c]
        xin[NPT * 128:] = ins["x_sample"][16 * c:16 * c + 16].reshape(128, D)
        in_maps.append(dict(
            xin=xin, ck=ck, cv=cv, pt=ins["page_table"][16 * c:16 * c + 16].reshape(1, -1).astype(np.int32),
            srw=np.ascontiguousarray(ins["state_rwkv"][:, 16 * c:16 * c + 16]), ssh=np.ascontiguousarray(ins["state_shift"][:, 16 * c:16 * c + 16]),
            relb=ins["rel_bias"].reshape(1, 128).astype(np.float32),
            w_in=ins["w_in"], w_a=ins["w_a_out"], w_b=ins["w_b_out"], w_o=ins["w_o"], w2=ins["w2"], a2=ins["a2"],
            vec=vec, lamv=lamv, bm_in=bm_in, mk_in=mk_in, id_in=np.eye(128, dtype=np.float32), cmk_in=cmk_np, cmc_in=cmc_np))
    _r = run_bass_kernel_spmd(nc, in_maps, core_ids=list(range(ncores)), **({'trace': True} if os.environ.get('KTRACE') else {}))
    if os.environ.get('KTRACE'):
        print('EXEC_TIME_NS', _r.exec_time_ns)
    res = _r.results
    T = SEQ + 16
    yp = np.stack([r["y_o"][128:128 + SEQ] for r in res])
    ys = np.concatenate([r["y_o"][NPT * 128:].reshape(16, 8, D) for r in res])
    kp = np.stack([r["k_o"][:, 112:112 + T].reshape(2, T, 4, 128) for r in res], axis=1)
    vp = np.stack([r["v_o"][:, 112:112 + T].reshape(2, T, 4, 128) for r in res], axis=1)
    sp = np.stack([r["sp_o"] for r in res], axis=1)
    hp = np.stack([r["hp_o"].reshape(2, D) for r in res], axis=1)
    ks = np.concatenate([r["k_o"][:, NPT * 128:].reshape(2, 16, 8, 4, 128) for r in res], axis=1)
    vs = np.concatenate([r["v_o"][:, NPT * 128:].reshape(2, 16, 8, 4, 128) for r in res], axis=1)
    ss = np.concatenate([r["ss_o"] for r in res], axis=1)
    hs = np.concatenate([r["hs_o"] for r in res], axis=1)
    return (yp, ys, kp, vp, sp, hp, ks, vs, ss, hs)
```
